# Optimizing a Trainium2 kernel written in Bass

```python
import math
import jax, jax.numpy as jnp
from jax import lax
import numpy as np

D_MODEL = 4096
BATCH = 4
SEQ = 2048
DEPTH = 1

ATTN_HEADS = 16
HEAD_DIM = 128
ATTN_WIDTH = ATTN_HEADS * HEAD_DIM
MOBA_BLOCK = 256
MOBA_TOPK = 3
Q_CHUNK = 16
REL_BUCKETS = 32
REL_MAX_DIST = 128
CONV_GROUPS = 16
CONV_GROUP_DIM = 128
CONV_WIDTH = CONV_GROUPS * CONV_GROUP_DIM
CONV_K = 3
IN_PROJ_WIDTH = 3 * ATTN_WIDTH + 3 * CONV_WIDTH + 2 * D_MODEL
IN_SPLITS = (ATTN_WIDTH, 2 * ATTN_WIDTH, 3 * ATTN_WIDTH,
             3 * ATTN_WIDTH + CONV_WIDTH, 3 * ATTN_WIDTH + 2 * CONV_WIDTH,
             3 * ATTN_WIDTH + 3 * CONV_WIDTH, 3 * ATTN_WIDTH + 3 * CONV_WIDTH + D_MODEL)
PEER_HEADS = 8
PEER_N_KEYS = 128
PEER_N_EXPERTS = PEER_N_KEYS * PEER_N_KEYS
PEER_QUERY_DIM = 256
PEER_HALF = PEER_QUERY_DIM // 2
PEER_TOPK = 16
PEER_TOKEN_CHUNK = 64
NORM_EPS = 1e-6

kernel_name = "hybrid_conv_moba_peer_block"


def rms_norm(x, g):
    x32 = x.astype(jnp.float32)
    y = x32 * lax.rsqrt(jnp.mean(x32 * x32, axis=-1, keepdims=True) + NORM_EPS)
    return y.astype(x.dtype) * g


def t5_bucket(dist):
    max_exact = REL_BUCKETS // 2
    d32 = jnp.maximum(dist, 1).astype(jnp.float32)
    large = max_exact + (jnp.log(d32 / max_exact) / math.log(REL_MAX_DIST / max_exact)
                         * (REL_BUCKETS - max_exact)).astype(jnp.int32)
    large = jnp.minimum(large, REL_BUCKETS - 1)
    return jnp.where(dist < max_exact, dist, large)


def short_conv_mixer(cb, cc, cu, conv_w, conv_b):
    gated = cc * cu
    y = lax.conv_general_dilated(gated, conv_w.astype(gated.dtype), window_strides=(1,),
                                 padding=[(CONV_K - 1, 0)],
                                 dimension_numbers=('NWC', 'WIO', 'NWC'),
                                 feature_group_count=CONV_WIDTH)
    return cb * (y + conv_b)


def moba_attention(q, k, v, rel_table):
    B, T = q.shape[0], q.shape[1]
    L = MOBA_BLOCK
    nb = -(-T // L)
    t_pad = nb * L
    pad = ((0, 0), (0, t_pad - T), (0, 0), (0, 0))
    q, k, v = [jnp.pad(a, pad).transpose(0, 2, 1, 3) for a in (q, k, v)]
    k_blocks = k.reshape(B, ATTN_HEADS, nb, L, HEAD_DIM)
    v_blocks = v.reshape(B, ATTN_HEADS, nb, L, HEAD_DIM)
    k_mean = jnp.mean(k_blocks.astype(jnp.float32), axis=3)
    gate = jnp.einsum('bhtd,bhnd->bhtn', q.astype(jnp.float32), k_mean)
    pos = jnp.arange(t_pad, dtype=jnp.int32)
    q_block = pos // L
    fully_past = jnp.arange(nb, dtype=jnp.int32)[None, :] < q_block[:, None]
    gate = jnp.where(fully_past, gate, -jnp.inf)
    k_sel = min(MOBA_TOPK, nb)
    _, top_idx = lax.top_k(gate, k_sel)
    top_idx = top_idx.astype(jnp.int32)
    top_valid = top_idx < q_block[:, None]
    own = jnp.broadcast_to(q_block[:, None], (B, ATTN_HEADS, t_pad, 1))
    sel = jnp.concatenate([top_idx, own], axis=-1)
    valid = jnp.concatenate([top_valid, jnp.ones_like(own, dtype=bool)], axis=-1)

    n_chunks = t_pad // Q_CHUNK

    def to_chunks(a):
        return jnp.moveaxis(a.reshape(B, ATTN_HEADS, n_chunks, Q_CHUNK, a.shape[-1]), 2, 0)

    b_idx = jnp.arange(B)[:, None, None, None]
    h_idx = jnp.arange(ATTN_HEADS)[None, :, None, None]
    key_off = jnp.arange(L, dtype=jnp.int32)
    scale = HEAD_DIM ** -0.5

    def attend_chunk(args):
        q_c, sel_c, valid_c, c = args
        k_g = k_blocks[b_idx, h_idx, sel_c]
        v_g = v_blocks[b_idx, h_idx, sel_c]
        logits = jnp.einsum('bhqd,bhqskd->bhqsk', q_c, k_g).astype(jnp.float32) * scale
        q_pos = c * Q_CHUNK + jnp.arange(Q_CHUNK, dtype=jnp.int32)
        k_pos = sel_c[..., None] * L + key_off
        dist = q_pos[None, None, :, None, None] - k_pos
        bias = rel_table[t5_bucket(jnp.maximum(dist, 0)), h_idx[..., None]]
        allowed = valid_c[..., None] & (dist >= 0)
        logits = jnp.where(allowed, logits + bias.astype(jnp.float32), -jnp.inf)
        probs = jax.nn.softmax(logits.reshape(B, ATTN_HEADS, Q_CHUNK, -1), axis=-1)
        probs = probs.reshape(logits.shape).astype(v_g.dtype)
        return jnp.einsum('bhqsk,bhqskd->bhqd', probs, v_g)

    out = lax.map(attend_chunk, (to_chunks(q), to_chunks(sel), to_chunks(valid),
                                 jnp.arange(n_chunks, dtype=jnp.int32)))
    out = jnp.moveaxis(out, 0, 2).reshape(B, ATTN_HEADS, t_pad, HEAD_DIM)
    out = out.transpose(0, 2, 1, 3)[:, :T]
    return out.reshape(B, T, ATTN_WIDTH)


def peer_ffn(x, w_q, sub_keys, u_table, v_table):
    B, T, D = x.shape
    xf = x.reshape(B * T, D)
    q = (xf @ w_q).reshape(-1, PEER_HEADS, PEER_QUERY_DIM).astype(jnp.float32)
    keys = sub_keys.astype(jnp.float32)
    s1 = jnp.einsum('nhd,hkd->nhk', q[..., :PEER_HALF], keys[:, 0])
    s2 = jnp.einsum('nhd,hkd->nhk', q[..., PEER_HALF:], keys[:, 1])
    v1, i1 = lax.top_k(s1, PEER_TOPK)
    v2, i2 = lax.top_k(s2, PEER_TOPK)
    cand = (v1[..., :, None] + v2[..., None, :]).reshape(-1, PEER_HEADS, PEER_TOPK * PEER_TOPK)
    score, c = lax.top_k(cand, PEER_TOPK)
    e1 = jnp.take_along_axis(i1, c // PEER_TOPK, axis=-1)
    e2 = jnp.take_along_axis(i2, c % PEER_TOPK, axis=-1)
    experts = (e1 * PEER_N_KEYS + e2).astype(jnp.int32)
    gates = jax.nn.softmax(score, axis=-1)
    n_chunks = xf.shape[0] // PEER_TOKEN_CHUNK

    def chunk(args):
        x_c, e_c, g_c = args
        u = u_table[e_c]
        a = jnp.einsum('cd,chkd->chk', x_c, u)
        act = jax.nn.gelu(a.astype(jnp.float32), approximate=False) * g_c
        return jnp.einsum('chk,chkd->cd', act.astype(x_c.dtype), v_table[e_c])

    y = lax.map(chunk, (xf.reshape(n_chunks, PEER_TOKEN_CHUNK, D),
                        experts.reshape(n_chunks, PEER_TOKEN_CHUNK, PEER_HEADS, PEER_TOPK),
                        gates.reshape(n_chunks, PEER_TOKEN_CHUNK, PEER_HEADS, PEER_TOPK)))
    return y.reshape(B, T, D)


def setup_inputs(seed: int = 0) -> dict:
    key = jax.random.key(seed)
    ks = jax.random.split(key, 17)
    f32 = jnp.float32
    D = D_MODEL
    nrm = lambda k, shape, s: jax.random.normal(k, shape, f32) * s
    return {
        "x": nrm(ks[0], (BATCH, SEQ, D), 1.0),
        "norm_mix": 1.0 + nrm(ks[1], (DEPTH, D), 0.02),
        "w_in": nrm(ks[2], (DEPTH, D, IN_PROJ_WIDTH), D ** -0.5),
        "conv_w": nrm(ks[3], (DEPTH, CONV_K, 1, CONV_WIDTH), CONV_K ** -0.5),
        "conv_b": nrm(ks[4], (DEPTH, CONV_WIDTH), 0.01),
        "w_br_attn": nrm(ks[5], (DEPTH, ATTN_WIDTH, D), ATTN_WIDTH ** -0.5),
        "w_br_conv": nrm(ks[6], (DEPTH, CONV_WIDTH, D), CONV_WIDTH ** -0.5),
        "b_gate": nrm(ks[7], (DEPTH, 2, D), 0.01),
        "rel_bias": nrm(ks[8], (REL_BUCKETS, ATTN_HEADS), 0.1),
        "w_out": nrm(ks[9], (DEPTH, D, D), D ** -0.5),
        "norm_ffn": 1.0 + nrm(ks[10], (DEPTH, D), 0.02),
        "peer_w_q": nrm(ks[11], (DEPTH, D, PEER_HEADS * PEER_QUERY_DIM), D ** -0.5),
        "peer_sub_keys": nrm(ks[12], (DEPTH, PEER_HEADS, 2, PEER_N_KEYS, PEER_HALF), PEER_HALF ** -0.5),
        "peer_u": nrm(ks[13], (DEPTH, PEER_N_EXPERTS, D), D ** -0.5),
        "peer_v": nrm(ks[14], (DEPTH, PEER_N_EXPERTS, D), (PEER_HEADS * PEER_TOPK) ** -0.5),
        "norm_final": 1.0 + nrm(ks[15], (D,), 0.02),
    }


def reference(x, norm_mix, w_in, conv_w, conv_b, w_br_attn, w_br_conv, b_gate, rel_bias,
              w_out, norm_ffn, peer_w_q, peer_sub_keys, peer_u, peer_v, norm_final):
    B, T, _ = x.shape
    h = x
    for l in range(DEPTH):
        hn = rms_norm(h, norm_mix[l])
        proj = hn @ w_in[l]
        q, k, v, cb, cc, cu, g_attn, g_conv = jnp.split(proj, IN_SPLITS, axis=-1)
        heads = lambda a: a.reshape(B, T, ATTN_HEADS, HEAD_DIM)
        z_attn = moba_attention(heads(q), heads(k), heads(v), rel_bias) @ w_br_attn[l]
        z_conv = short_conv_mixer(cb, cc, cu, conv_w[l], conv_b[l]) @ w_br_conv[l]
        merged = (jax.nn.sigmoid(g_attn + b_gate[l, 0]) * z_attn
                  + jax.nn.sigmoid(g_conv + b_gate[l, 1]) * z_conv)
        h = h + merged @ w_out[l]
        h = h + peer_ffn(rms_norm(h, norm_ffn[l]), peer_w_q[l], peer_sub_keys[l],
                         peer_u[l], peer_v[l])
    return rms_norm(h, norm_final)
```

```python
import math
from contextlib import ExitStack
import numpy as np
import concourse.bass as bass
import concourse.mybir as mybir
from concourse.bass_utils import run_bass_kernel_spmd

F32 = mybir.dt.float32
BF16 = mybir.dt.bfloat16
AF = mybir.ActivationFunctionType
ALU = mybir.AluOpType
AX = mybir.AxisListType

SEM_LIMIT = 24000
ENGS = ("pe", "act", "dve", "pool", "sp")


class Res:
    __slots__ = ("name", "w", "r", "hz")

    def __init__(self, name):
        self.name = name
        self.w = None
        self.r = {}
        self.hz = False


class Prog:
    def __init__(self, nc, stack, n_sems=100):
        self.nc = nc
        self.sems = [stack.enter_context(nc.semaphore(f"s{i}")) for i in range(n_sems)]
        self.next_sem = 0
        self.q = {e: [] for e in ENGS}
        self.esem = {}
        self.ecnt = {}
        self.allsem = {}
        for e in ENGS:
            self._new_esem(e)
        self.waited = {e: {} for e in ENGS}
        self.ksem = {}
        self.free_ks = []

    def _alloc_sem(self):
        i = self.next_sem
        self.next_sem += 1
        assert i < len(self.sems), "out of semaphores"
        return i

    def _new_esem(self, e):
        self.esem[e] = self._alloc_sem()
        self.ecnt[e] = 0

    def _deps(self, eng, reads, writes, skip_own=False, sreads=()):
        need = {}
        own = self.esem[eng]
        strict_own = 0

        def add(s, v):
            if need.get(s, 0) < v:
                need[s] = v
        for r in reads:
            if r.w is not None:
                add(*r.w)
                if r.w[0] == own:
                    strict_own = max(strict_own, r.w[1])
        for r in sreads:
            if r.w is not None:
                add(*r.w)
                if r.w[0] == own:
                    strict_own = max(strict_own, r.w[1])
        for w in writes:
            if w.w is not None:
                add(*w.w)
            for s, v in w.r.items():
                add(s, v)
        if own in need:
            if skip_own or strict_own == 0:
                del need[own]
            else:
                need[own] = strict_own
        return self._filter(eng, need, False)

    def _filter(self, eng, need, skip_own=False):
        out = []
        wd = self.waited[eng]
        for s, v in need.items():
            if skip_own and s == self.esem[eng]:
                continue
            if wd.get(s, 0) >= v:
                continue
            wd[s] = v
            out.append((s, v))
        return out

    def _commit(self, ev, reads, writes):
        s, v = ev
        self.allsem[s] = max(self.allsem.get(s, 0), v)
        for r in reads:
            if r.r.get(s, 0) < v:
                r.r[s] = v
        for w in writes:
            w.w = ev
            w.r = {}

    def op(self, eng, fn, reads=(), writes=(), sreads=(), hz=False):
        if self.ecnt[eng] >= SEM_LIMIT:
            self._new_esem(eng)
        waits = self._deps(eng, reads, writes, skip_own=(eng == "pe"), sreads=sreads)
        self.ecnt[eng] += 1
        ev = (self.esem[eng], self.ecnt[eng])
        self.q[eng].append((waits, fn, ev[0], 1))
        self._commit(ev, list(reads) + list(sreads), writes)
        if hz:
            for w in writes:
                w.hz = True
        return ev

    def dma(self, queue, pieces, reads=(), writes=(), key=None):
        ks = self.ksem.get(key)
        if ks is None or ks[1] + 16 * len(pieces) >= SEM_LIMIT:
            while self.free_ks and self.free_ks[-1][1] + 16 * len(pieces) >= SEM_LIMIT:
                self.free_ks.pop()
            ks = self.free_ks.pop() if self.free_ks else [self._alloc_sem(), 0]
            self.ksem[key] = ks
        waits = self._deps(queue, reads, writes)
        for i, (o, a) in enumerate(pieces):
            fn = (lambda e, o=o, a=a: e.dma_start(out=o, in_=a))
            self.q[queue].append((waits if i == 0 else [], fn, ks[0], 16))
        ks[1] += 16 * len(pieces)
        ev = (ks[0], ks[1])
        self._commit(ev, reads, writes)
        return ev

    def barrier(self):
        for e in ENGS:
            need = dict(self.allsem)
            waits = self._filter(e, need, skip_own=True)
            if waits:
                self.q[e].append((waits, None, None, 0))
        self.free_ks.extend(self.ksem.values())
        self.ksem = {}

    def emit(self):
        nc = self.nc
        sems = self.sems
        engmap = {"pe": "tensor", "act": "scalar", "dve": "vector", "pool": "gpsimd", "sp": "sync"}
        with nc.Block() as block:
            for e in ENGS:
                ops = self.q[e]
                if not ops:
                    continue

                def body(eng, ops=ops):
                    for waits, fn, s, inc in ops:
                        for (ws, wv) in waits:
                            eng.wait_ge(sems[ws], wv)
                        if fn is not None:
                            fn(eng).then_inc(sems[s], inc)
                getattr(block, engmap[e])(body)


D = 4096
T = 1024
NCH = 32
SCALE = 128 ** -0.5
EPS = 1e-6
BIGNEG = -30000.0
W_IN = 20480


class Builder:
    def __init__(self, stage=99, debug=False):
        self.stage = stage
        self.debug = debug
        self.nc = bass.Bass("TRN2", target_bir_lowering=False)
        self.st = ExitStack()
        self.R = {}
        self.bi = 0
        self.evi = 0

    def res(self, key):
        r = self.R.get(key)
        if r is None:
            r = Res(str(key))
            self.R[key] = r
        return r

    def din(self, name, shape):
        return self.nc.dram_tensor(name, list(shape), F32, kind="ExternalInput").ap()

    def dscr(self, name, shape, dt):
        kind = "ExternalOutput" if (self.debug and name in self.debug) else "Internal"
        return self.nc.dram_tensor(name, list(shape), dt, kind=kind).ap()

    def carve(self, nwords, dt=F32, shape=None):
        v = self.arena[:, self.off:self.off + nwords]
        self.off += nwords
        assert self.off <= self.arena_words, f"arena overflow {self.off}"
        if dt != F32:
            v = v.bitcast(dt)
        if shape is not None:
            if len(shape) == 2:
                v = v.rearrange("p (a b) -> p a b", a=shape[0])
            elif len(shape) == 3:
                v = v.rearrange("p (a b c) -> p a b c", a=shape[0], b=shape[1])
        return v

    def bank(self):
        b = self.banks[self.bi % 8]
        self.bi += 1
        return b

    def eveng(self):
        self.evi += 1
        return "act" if self.evi % 2 == 0 else "dve"

    def mm(self, out, lhsT, rhs, start, stop, reads, writes):
        self.P.op("pe", lambda e: e.matmul(out, lhsT, rhs, start=start, stop=stop), reads, writes)

    def tr(self, out, in_, reads, writes):
        idn = self.ident16
        self.P.op("pe", lambda e: e.transpose(out, in_, idn), list(reads) + [self.res("const")], writes)

    def A(self, out, in_, func, reads, writes, bias=None, scale=None, accum=None):
        kw = {}
        if bias is not None:
            kw["bias"] = bias
        if scale is not None:
            kw["scale"] = scale
        if accum is not None:
            kw["accum_out"] = accum
        self.P.op("act", lambda e: e.activation(out=out, in_=in_, func=func, **kw), reads, writes, hz=(accum is not None))

    def copy(self, eng, out, in_, reads, writes):
        if eng == "act":
            self.A(out, in_, AF.Copy, reads, writes)
        else:
            self.P.op(eng, lambda e: e.tensor_copy(out=out, in_=in_), reads, writes)

    def tt(self, eng, out, in0, in1, op, reads, writes):
        self.P.op(eng, lambda e: e.tensor_tensor(out=out, in0=in0, in1=in1, op=op), reads, writes)

    def ts(self, eng, out, in0, s1, s2, op0, op1, reads, writes, sreads=()):
        if op1 is None:
            self.P.op(eng, lambda e: e.tensor_scalar(out=out, in0=in0, scalar1=s1, scalar2=None, op0=op0), reads, writes, sreads=sreads)
        else:
            self.P.op(eng, lambda e: e.tensor_scalar(out=out, in0=in0, scalar1=s1, scalar2=s2, op0=op0, op1=op1), reads, writes, sreads=sreads)

    def stt(self, out, in0, scalar, in1, op0, op1, reads, writes, sreads=()):
        self.P.op("dve", lambda e: e.scalar_tensor_tensor(out=out, in0=in0, scalar=scalar, in1=in1, op0=op0, op1=op1),
                  reads, writes, sreads=sreads)

    def red(self, out, in_, op, reads, writes):
        self.P.op("dve", lambda e: e.tensor_reduce(out=out, in_=in_, axis=AX.X, op=op), reads, writes)

    def load(self, out, in_, res, reads=(), queue="sp"):
        self.P.dma(queue, [(out, in_)], reads=reads, writes=[res], key=res.name)

    def store(self, out, in_, dres, sres):
        self.P.dma("sp", [(out, in_)], reads=[sres], writes=[dres], key="st_" + sres.name)

    def build(self):
        nc = self.nc
        st = self.st
        with st:
            self.P = Prog(nc, st)
            self.arena_words = 51200
            self.arena = st.enter_context(nc.sbuf_tensor("arena", [128, self.arena_words], F32))
            self.banks = []
            for i in range(8):
                t = st.enter_context(nc.psum_tensor(f"bank{i}", [128, 512], F32))
                self.banks.append((t, Res(f"bank{i}")))
            self.off = 0
            self._declare_io()
            self._consts()
            self.base = self.off
            stages = [self.phase12, self.phase3, self.phase4, self.phase5, self.phase6]
            for i, f in enumerate(stages):
                if self.stage > i:
                    self.off = self.base
                    f()
                    self.P.barrier()
            self.P.barrier()
            self.P.emit()
        return nc

    def _declare_io(self):
        d = self.din
        self.xctx = d("xctx", [2048, D])
        self.xT = d("xT", [D, T])
        self.w_in = d("w_in", [D, W_IN])
        self.w_bra = d("w_bra", [2048, D])
        self.w_brc = d("w_brc", [2048, D])
        self.w_out = d("w_out", [D, D])
        self.w_q = d("w_q", [D, 2048])
        self.uT = d("uT", [D, 16384])
        self.pv = d("pv", [16384, D])
        self.keysT = d("keysT", [128, 16 * 128])
        self.nearb = d("nearb", [2 * 16 * 128, 512])
        self.smallc = d("smallc", [128, 1600])
        self.gmixb = d("gmixb", [1, D])
        self.identd = d("identd", [128, 128])
        s = self.dscr
        self.qT_s = s("qT_s", [2048, T], BF16)
        self.kT_s = s("kT_s", [2048, 2048], BF16)
        self.v_s = s("v_s", [2048, 2048], BF16)
        self.convT_s = s("convT_s", [2048, T], BF16)
        self.gA_s = s("gA_s", [D, T], F32)
        self.gC_s = s("gC_s", [D, T], F32)
        self.attnT_s = s("attnT_s", [2048, T], BF16)
        self.h1T_s = s("h1T_s", [D, T], F32)
        self.hn2T_s = s("hn2T_s", [D, T], BF16)
        self.GT_s = s("GT_s", [16384, T], F32)
        self.actT_s = s("actT_s", [16384, T], BF16)
        self.h2T_s = s("h2T_s", [D, T], F32)
        self.outT = nc_out = self.nc.dram_tensor("outT", [D, T], F32, kind="ExternalOutput").ap()

    def _consts(self):
        P = self.P
        c = self.res("const")
        self.ident16 = self.carve(64, BF16)
        self.keys16 = self.carve(1024, BF16, shape=(16, 128))
        self.small = self.carve(1600)
        sm = self.small
        o = 0

        def take(n, shape=None):
            nonlocal o
            v = sm[:, o:o + n]
            o += n
            if shape is not None:
                if len(shape) == 2:
                    v = v.rearrange("p (a b) -> p a b", a=shape[0])
            return v
        self.gmix = take(32)
        self.gffn = take(32)
        self.gfin = take(32)
        self.convw = take(48, (16, 3))
        self.convb = take(16)
        self.bgate = take(64, (2, 32))
        self.cfar = take(16)
        self.pastb = take(64, (8, 8))
        self.valid01 = take(64, (8, 8))
        self.own01 = take(64, (8, 8))
        self.cmask = take(1024, (2, 512))
        self.ones32 = take(128)
        self.epsc = take(1)
        P.dma("pool", [(self.ident16, self.identd)], writes=[c], key="c_id")
        P.dma("pool", [(self.keys16, self.keysT.rearrange("p (a b) -> p a b", a=16))], writes=[c], key="c_keys")
        P.dma("sp", [(self.small, self.smallc)], writes=[c], key="c_small")
        self.misc = self.carve(512)
        P.barrier()

    def phase12(self):
        P = self.P
        hnT = self.carve(16384, BF16, shape=(32, 1024))
        hnr = self.res("hnT")
        hn_tail = self.carve(32, BF16, shape=(32, 2))
        tailr = self.res("hn_tail")
        mark = self.off
        xts = [(self.carve(4096), self.res(f"xt{i}")) for i in range(2)]
        gb = self.carve(4096)
        gbr = self.res("gb")
        xs = self.carve(2048, BF16)
        xsr = self.res("xs")
        stat = self.misc
        ssq, std, rstd = stat[:, 0:1], stat[:, 1:2], stat[:, 2:3]
        sr = self.res("stat")
        self.load(gb, self.gmixb.broadcast_to([128, D]), gbr)
        norm_end = self.off
        self.off = mark
        wts = [(self.carve(4096, BF16), self.res(f"wt{i}")) for i in range(2)]
        stg = [(self.carve(1024), self.res(f"stg{i}")) for i in range(3)]
        ccs = self.carve(1032)
        gated = self.carve(1032)
        yb = self.carve(1024)
        ccr, gr, yr = self.res("ccs"), self.res("gated"), self.res("yb")
        vwts = [(self.carve(8192, BF16, shape=(32, 512)), self.res(f"vwt{i}")) for i in range(2)]
        self.vwi = 0
        self.off = max(self.off, norm_end)
        self.wts = wts
        self.wi = 0
        self.si = 0
        cst = self.res("const")

        def norm_seg(seg):
            for tt in range(8):
                xt, xr = xts[tt % 2]
                self.load(xt, self.xctx[seg * 1024 + tt * 128: seg * 1024 + (tt + 1) * 128, :], xr)
                self.A(xs, xt, AF.Square, [xr], [xsr, sr], accum=ssq)
                self.A(std, ssq, AF.Sqrt, [sr, cst], [sr], scale=1.0 / D, bias=self.epsc)
                P.op("dve", lambda e: e.reciprocal(out=rstd, in_=std), [sr], [sr])
                self.stt(xs, xt, rstd, gb, ALU.mult, ALU.mult, [xr, gbr], [xsr], sreads=[sr])
                for cg in range(4):
                    bk, br = self.bank()
                    bk16 = bk[:].bitcast(BF16)
                    for j in range(8):
                        cc = cg * 8 + j
                        self.tr(bk16[:, j * 128:(j + 1) * 128], xs[:, cc * 128:(cc + 1) * 128], [xsr], [br])
                    self.copy(self.eveng(), hnT[:, cg * 8:(cg + 1) * 8, tt * 128:(tt + 1) * 128],
                              bk16.rearrange("p (c t) -> p c t", c=8), [br], [hnr])

        def wtile(KC, ncols):
            wt, wr = wts[self.wi % 2]
            self.wi += 1
            return wt[:, 0:KC * ncols].rearrange("p (k n) -> p k n", k=KC), wr
        self.wtile = wtile

        def stage_slot():
            s = stg[self.si % 3]
            self.si += 1
            return s

        def proj_fm(W, c0, ncols, epi, wcols=256, act=hnT, actr=hnr, KC=32, NT=1024):
            for g0 in range(c0, c0 + ncols, wcols):
                wt, wr = wtile(KC, wcols)
                P.dma("pool", [(wt, W[:, g0:g0 + wcols].rearrange("(kc p) n -> p kc n", p=128))], writes=[wr], key=wr.name)
                for m in range(wcols // 128):
                    for nh in range(NT // 512):
                        bk, br = self.bank()
                        for kc in range(KC):
                            self.mm(bk[:], wt[:, kc, m * 128:(m + 1) * 128], act[:, kc, nh * 512:(nh + 1) * 512],
                                    kc == 0, kc == KC - 1, [wr, actr], [br])
                        epi(g0 + m * 128, nh, bk, br)
        self.proj_fm = proj_fm
        self.stage_slot = stage_slot

        km32 = self.misc[:, 16:144].rearrange("p (h n) -> p h n", h=16)
        kmr = self.res("km")

        def epi_store_bf16(dst, col_base, tok0, kmeans=False):
            state = {}

            def epi(c0, nh, bk, br):
                if nh == 0:
                    state["slot"] = stage_slot()
                sl, slr = state["slot"]
                s16 = sl.bitcast(BF16)
                self.copy(self.eveng(), s16[:, nh * 512:(nh + 1) * 512], bk[:], [br], [slr])
                if nh == 1:
                    r0 = c0 - col_base
                    if kmeans:
                        hh = r0 // 128
                        b0 = tok0 // 256
                        self.red(km32[:, hh, b0:b0 + 4], s16[:, 0:1024].rearrange("p (n l) -> p n l", n=4), ALU.add, [slr], [kmr])
                    self.store(dst[r0:r0 + 128, tok0:tok0 + 1024], s16[:, 0:1024], self.res(("d", id(dst), r0)), slr)
            return epi

        def proj_v(seg):
            for g0 in range(4096, 6144, 512):
                wt, wr = vwts[self.vwi % 2]
                self.vwi += 1
                P.dma("pool", [(wt, self.w_in[:, g0:g0 + 512].rearrange("(kc p) n -> p kc n", p=128))], writes=[wr], key=wr.name)
                for th in range(2):
                    sl, slr = stage_slot()
                    s16 = sl.bitcast(BF16).rearrange("p (t n) -> p t n", t=4)
                    for t4 in range(4):
                        tt = th * 4 + t4
                        bk, br = self.bank()
                        for kc in range(32):
                            self.mm(bk[:], hnT[:, kc, tt * 128:(tt + 1) * 128], wt[:, kc, :], kc == 0, kc == 31, [wr, hnr], [br])
                        self.copy(self.eveng(), s16[:, t4, :], bk[:], [br], [slr])
                    r0 = seg * 1024 + th * 512
                    self.store(self.v_s[r0:r0 + 512, g0 - 4096:g0 - 4096 + 512].rearrange("(t p) n -> p t n", p=128),
                               s16, self.res(("v", seg, g0, th)), slr)

        norm_seg(0)
        self.copy("dve", hn_tail, hnT[:, :, 1022:1024], [hnr], [tailr])
        P.barrier()
        proj_fm(self.w_in, 2048, 2048, epi_store_bf16(self.kT_s, 2048, 0, kmeans=True))
        proj_v(0)
        P.barrier()
        self.load(gb, self.gmixb.broadcast_to([128, D]), gbr)
        norm_seg(1)
        P.barrier()
        proj_fm(self.w_in, 2048, 2048, epi_store_bf16(self.kT_s, 2048, 1024, kmeans=True))
        proj_v(1)
        proj_fm(self.w_in, 0, 2048, epi_store_bf16(self.qT_s, 0, 0))
        for ch in range(16):
            pcs = {}
            for nm, cbase in (("cc", 8192), ("cu", 10240), ("cb", 6144)):
                wt, wr = wtile(32, 128)
                g0 = cbase + ch * 128
                P.dma("pool", [(wt, self.w_in[:, g0:g0 + 128].rearrange("(kc p) n -> p kc n", p=128))], writes=[wr], key=wr.name)
                for nh in range(2):
                    bk, br = self.bank()
                    for kc in range(32):
                        self.mm(bk[:], wt[:, kc, :], hnT[:, kc, nh * 512:(nh + 1) * 512], kc == 0, kc == 31, [wr, hnr], [br])
                    pcs[(nm, nh)] = (bk, br)
                if nm != "cb":
                    bk, br = self.bank()
                    for kc in range(32):
                        self.mm(bk[:, 0:2], wt[:, kc, :], hn_tail[:, kc, :], kc == 0, kc == 31, [wr, tailr], [br])
                    pcs[(nm, 2)] = (bk, br)
                if nm == "cc":
                    for nh in range(2):
                        bk, br = pcs[("cc", nh)]
                        self.copy("act", ccs[:, 2 + nh * 512: 2 + (nh + 1) * 512], bk[:], [br], [ccr])
                    bk, br = pcs[("cc", 2)]
                    self.copy("act", ccs[:, 0:2], bk[:, 0:2], [br], [ccr])
                if nm == "cu":
                    for nh in range(2):
                        bk, br = pcs[("cu", nh)]
                        self.tt("dve", gated[:, 2 + nh * 512: 2 + (nh + 1) * 512], ccs[:, 2 + nh * 512: 2 + (nh + 1) * 512], bk[:],
                                ALU.mult, [br, ccr], [gr])
                    bk, br = pcs[("cu", 2)]
                    self.tt("dve", gated[:, 0:2], ccs[:, 0:2], bk[:, 0:2], ALU.mult, [br, ccr], [gr])
                    self.ts("dve", yb, gated[:, 2:1026], self.convw[:, ch, 2:3], self.convb[:, ch:ch + 1], ALU.mult, ALU.add,
                            [gr, cst], [yr])
                    self.stt(yb, gated[:, 1:1025], self.convw[:, ch, 1:2], yb, ALU.mult, ALU.add, [gr, cst, yr], [yr])
                    self.stt(yb, gated[:, 0:1024], self.convw[:, ch, 0:1], yb, ALU.mult, ALU.add, [gr, cst, yr], [yr])
                if nm == "cb":
                    sl, slr = stage_slot()
                    s16 = sl.bitcast(BF16)
                    for nh in range(2):
                        bk, br = pcs[("cb", nh)]
                        self.tt("dve", s16[:, nh * 512:(nh + 1) * 512], yb[:, nh * 512:(nh + 1) * 512], bk[:], ALU.mult,
                                [br, yr], [slr])
                    self.store(self.convT_s[ch * 128:(ch + 1) * 128, :], s16[:, 0:1024], self.res(("conv", ch)), slr)
        for which, dst in ((0, self.gA_s), (1, self.gC_s)):
            state = {}

            def epi(c0, nh, bk, br, which=which, dst=dst, state=state):
                if nh == 0:
                    state["slot"] = stage_slot()
                sl, slr = state["slot"]
                cidx = (c0 - 12288 - which * 4096) // 128
                self.A(sl[:, nh * 512:(nh + 1) * 512], bk[:], AF.Sigmoid, [br, cst], [slr], bias=self.bgate[:, which, cidx:cidx + 1])
                if nh == 1:
                    self.store(dst[cidx * 128:(cidx + 1) * 128, :], sl, self.res(("g", which, cidx)), slr)
            proj_fm(self.w_in, 12288 + which * 4096, 4096, epi)

    def phase3(self):
        P = self.P
        cst = self.res("const")
        qT = self.carve(8192, BF16, shape=(16, 1024))
        qr = self.res("qT")
        attnT = self.carve(8192, BF16, shape=(16, 1024))
        ar = self.res("attnT")
        negb = self.carve(1024, shape=(8, 16, 8))
        nbr = self.res("negb")
        km32 = self.misc[:, 16:144].rearrange("p (h n) -> p h n", h=16)
        km16 = self.carve(64, BF16, shape=(16, 8))
        kmr = self.res("km")
        kts = [(self.carve(1024, BF16), self.res(f"kt{i}")) for i in range(2)]
        vts = [(self.carve(1024, BF16, shape=(16, 128)), self.res(f"vt{i}")) for i in range(2)]
        nbs = [(self.carve(1024, shape=(2, 512)), self.res(f"nb{i}")) for i in range(2)]
        Ls = [(self.carve(2048), self.res(f"L{i}")) for i in range(3)]
        Ps = [(self.carve(1024, BF16), self.res(f"P{i}")) for i in range(3)]
        PTs = [(self.carve(1024, BF16, shape=(16, 128)), self.res(f"PT{i}")) for i in range(3)]
        gm = [self.carve(128, shape=(16, 8)) for _ in range(4)]
        gmr = self.res("gm")
        mx = self.carve(16)
        rs_all = self.carve(24)
        rsrs = [self.res(f"rs{i}") for i in range(3)]
        self.load(qT, self.qT_s.rearrange("(h p) t -> p h t", p=128), qr, reads=[self.res(("d", id(self.qT_s), r0)) for r0 in range(0, 2048, 128)])
        self.ts("dve", km16, km32, 1.0 / 256.0, None, ALU.mult, None, [kmr], [kmr])
        for i in range(8):
            bk, br = self.bank()
            for h in range(16):
                self.mm(bk[:, h * 8:(h + 1) * 8], qT[:, h, i * 128:(i + 1) * 128], km16[:, h, :], True, True, [qr, kmr], [br])
            pg = bk[:, 0:128].rearrange("p (h n) -> p h n", h=16)

            def bc(v):
                return v.unsqueeze(1).broadcast_to([128, 16, 8])

            def bm(v):
                return v.unsqueeze(2).broadcast_to([128, 16, 8])
            g0, g1, g2, eq = gm
            self.tt("dve", g0, pg, bc(self.pastb[:, i, :]), ALU.add, [br, cst], [gmr])
            self.red(mx, g0, ALU.max, [gmr], [gmr])
            self.tt("dve", eq, g0, bm(mx), ALU.is_equal, [gmr], [gmr])
            self.stt(g1, eq, -3e30, g0, ALU.mult, ALU.add, [gmr], [gmr])
            self.red(mx, g1, ALU.max, [gmr], [gmr])
            self.tt("dve", eq, g1, bm(mx), ALU.is_equal, [gmr], [gmr])
            self.stt(g2, eq, -3e30, g1, ALU.mult, ALU.add, [gmr], [gmr])
            self.red(mx, g2, ALU.max, [gmr], [gmr])
            self.tt("dve", eq, g0, bm(mx), ALU.is_ge, [gmr], [gmr])
            self.tt("dve", eq, eq, bc(self.valid01[:, i, :]), ALU.mult, [gmr, cst], [gmr])
            self.tt("dve", eq, eq, bc(self.own01[:, i, :]), ALU.add, [gmr, cst], [gmr])
            self.ts("dve", negb[:, i, :, :], eq, -1.0, -BIGNEG, ALU.add, ALU.mult, [gmr], [nbr])
        items = [(h, i) for h in range(16) for i in range(8)]
        bctr = {"s": 0, "t": 0, "v": 0}

        def pbank(kind):
            lo, n = {"s": (0, 4), "t": (4, 2), "v": (6, 2)}[kind]
            b = self.banks[lo + bctr[kind] % n]
            bctr[kind] += 1
            return b

        def stageA(idx):
            h, i = items[idx]
            kt, kr = kts[h % 2]
            nb, nr = nbs[h % 2]
            if i == 0:
                vt, vr = vts[h % 2]
                self.load(kt, self.kT_s[h * 128:(h + 1) * 128, :], kr)
                self.load(vt, self.v_s[:, h * 128:(h + 1) * 128].rearrange("(t p) d -> p t d", p=128), vr)
                self.load(nb, self.nearb.rearrange("(a h p) k -> p a h k", a=2, h=16)[:, :, h, :], nr)
                self.tt("dve", nb, nb, self.cmask, ALU.add, [nr, cst], [nr])
            cb = 4 + i // 2
            par = i % 2
            nkeys = (cb + 1) * 256
            L, Lr = Ls[idx % 3]
            Pb, Pr = Ps[idx % 3]
            rs = rs_all[:, (idx % 3) * 8:(idx % 3) * 8 + 8]
            rsr = rsrs[idx % 3]
            nbk = (nkeys + 511) // 512
            for b in range(nbk):
                w = min(512, nkeys - b * 512)
                bk, br = pbank("s")
                self.mm(bk[:, 0:w], qT[:, h, i * 128:(i + 1) * 128], kt[:, b * 512:b * 512 + w], True, True, [qr, kr], [br])
                nblk = w // 256
                self.stt(L[:, b * 512:b * 512 + w].rearrange("p (n l) -> p n l", n=nblk),
                         bk[:, 0:w].rearrange("p (n l) -> p n l", n=nblk), SCALE,
                         negb[:, i, h, 2 * b:2 * b + nblk].unsqueeze(2).broadcast_to([128, nblk, 256]),
                         ALU.mult, ALU.add, [br, nbr], [Lr])
            n0 = (cb - 1) * 256
            self.tt("dve", L[:, n0:n0 + 512], L[:, n0:n0 + 512], nb[:, par, :], ALU.add, [Lr, nr], [Lr])
            self.A(Pb[:, 0:n0], L[:, 0:n0], AF.Exp, [Lr, cst], [Pr, rsr], bias=self.cfar[:, h:h + 1], accum=rs[:, 0:1])
            self.A(Pb[:, n0:n0 + 512], L[:, n0:n0 + 512], AF.Exp, [Lr], [Pr, rsr], accum=rs[:, 1:2])

        def stageA2(idx):
            h, i = items[idx]
            nkeys = (4 + i // 2 + 1) * 256
            Pb, Pr = Ps[idx % 3]
            rs = rs_all[:, (idx % 3) * 8:(idx % 3) * 8 + 8]
            rsr = rsrs[idx % 3]
            self.tt("dve", rs[:, 2:3], rs[:, 0:1], rs[:, 1:2], ALU.add, [rsr], [rsr])
            P.op("dve", lambda e, rs=rs: e.reciprocal(out=rs[:, 3:4], in_=rs[:, 2:3]), [rsr], [rsr])
            self.ts("dve", Pb[:, 0:nkeys], Pb[:, 0:nkeys], rs[:, 3:4], None, ALU.mult, None, [Pr], [Pr], sreads=[rsr])

        def stageBT(idx):
            h, i = items[idx]
            cb = 4 + i // 2
            nkeys = (cb + 1) * 256
            Pb, Pr = Ps[idx % 3]
            PT, PTr = PTs[idx % 3]
            nchk = nkeys // 128
            for g in range((nchk + 7) // 8):
                n = min(8, nchk - g * 8)
                bk, br = pbank("t")
                bk16 = bk[:].bitcast(BF16)
                for j in range(n):
                    kc = g * 8 + j
                    self.tr(bk16[:, j * 128:(j + 1) * 128], Pb[:, kc * 128:(kc + 1) * 128], [Pr], [br])
                self.copy("act", PT[:, g * 8:g * 8 + n, :],
                          bk16[:, 0:n * 128].rearrange("p (c t) -> p c t", c=n), [br], [PTr])

        def stageBV(idx):
            h, i = items[idx]
            vt, vr = vts[h % 2]
            cb = 4 + i // 2
            nkeys = (cb + 1) * 256
            PT, PTr = PTs[idx % 3]
            nchk = nkeys // 128
            bk, br = pbank("v")
            for kc in range(nchk):
                self.mm(bk[:, 0:128], vt[:, kc, :], PT[:, kc, :], kc == 0, kc == nchk - 1, [vr, PTr], [br])
            self.copy("act", attnT[:, h, i * 128:(i + 1) * 128], bk[:, 0:128], [br], [ar])

        NI = len(items)
        stageA(0)
        stageA(1)
        stageA2(0)
        for idx in range(NI):
            stageBT(idx)
            if idx + 2 < NI:
                stageA(idx + 2)
            if idx + 1 < NI:
                stageA2(idx + 1)
            stageBV(idx)
        self.store(self.attnT_s.rearrange("(h p) t -> p h t", p=128), attnT, self.res("attnT_s"), ar)

    def phase4(self):
        P = self.P
        cst = self.res("const")
        mergedT = self.carve(16384, BF16, shape=(32, 1024))
        mr = self.res("mergedT")
        wbuf = self.carve(8192)
        ssqp = self.carve(1024)
        ssr = self.res("ssqp")
        mark = self.off
        A0 = self.carve(16384, BF16)
        attnT = A0[:, 0:16384].rearrange("p (h t) -> p h t", h=16)
        convT = A0[:, 16384:32768].rearrange("p (h t) -> p h t", h=16)
        a0r = self.res("A0")
        wbr = [(wbuf[:, i * 2048:(i + 1) * 2048].bitcast(BF16).rearrange("p (k n) -> p k n", k=16), self.res(f"wb{i}")) for i in range(4)]
        gts = [(self.carve(1024, shape=(2, 512)), self.res(f"gt{i}")) for i in range(2)]
        tmp = [(self.carve(512), self.res(f"tmp{i}")) for i in range(4)]
        self.load(attnT, self.attnT_s.rearrange("(h p) t -> p h t", p=128), a0r, reads=[self.res("attnT_s")])
        self.load(convT, self.convT_s.rearrange("(h p) t -> p h t", p=128), a0r,
                  reads=[self.res(("conv", ch)) for ch in range(16)])
        P.op("dve", lambda e: e.memset(ssqp, 0.0), [], [ssr])
        gi = 0
        for gidx, cg in enumerate(range(0, D, 256)):
            wa3, war = wbr[(gidx % 2) * 2]
            wc3, wcr = wbr[(gidx % 2) * 2 + 1]
            P.dma("pool", [(wa3, self.w_bra[:, cg:cg + 256].rearrange("(kc p) n -> p kc n", p=128))], writes=[war], key=war.name)
            P.dma("pool", [(wc3, self.w_brc[:, cg:cg + 256].rearrange("(kc p) n -> p kc n", p=128))], writes=[wcr], key=wcr.name)
            for m in range(2):
                c = cg // 128 + m
                for nh in range(2):
                    gt, gtr = gts[gi % 2]
                    gi += 1
                    tsl = slice(nh * 512, (nh + 1) * 512)
                    P.dma("sp", [(gt[:, 0, :], self.gA_s[c * 128:(c + 1) * 128, tsl]), (gt[:, 1, :], self.gC_s[c * 128:(c + 1) * 128, tsl])],
                          reads=[self.res(("g", 0, c)), self.res(("g", 1, c))], writes=[gtr], key=gtr.name)
                    ba, bar = self.bank()
                    for kc in range(16):
                        self.mm(ba[:], wa3[:, kc, m * 128:(m + 1) * 128], attnT[:, kc, tsl], kc == 0, kc == 15, [war, a0r], [bar])
                    bc_, bcr = self.bank()
                    for kc in range(16):
                        self.mm(bc_[:], wc3[:, kc, m * 128:(m + 1) * 128], convT[:, kc, tsl], kc == 0, kc == 15, [wcr, a0r], [bcr])
                    t0, t0r = tmp[(gi % 2) * 2]
                    t1, t1r = tmp[(gi % 2) * 2 + 1]
                    self.tt("dve", t0, ba[:], gt[:, 0, :], ALU.mult, [bar, gtr], [t0r])
                    self.tt("dve", t1, bc_[:], gt[:, 1, :], ALU.mult, [bcr, gtr], [t1r])
                    self.tt("dve", mergedT[:, c, tsl], t0, t1, ALU.add, [t0r, t1r], [mr])
        P.barrier()
        self.off = mark
        wts = [(wbuf[:, i * 4096:(i + 1) * 4096].bitcast(BF16).rearrange("p (k n) -> p k n", k=32), self.res(f"wo{i}")) for i in range(2)]
        xts = [(self.carve(1024), self.res(f"x4_{i}")) for i in range(2)]
        hst = [(self.carve(1024), self.res(f"h4_{i}")) for i in range(2)]
        h16 = [(self.carve(512), self.res(f"h16_{i}")) for i in range(2)]
        sq = self.carve(1024)
        sqr = self.res("sq")
        rstd = self.carve(1024)
        rr = self.res("rstd")
        for gidx, cg in enumerate(range(0, D, 256)):
            w3, wr = wts[gidx % 2]
            P.dma("pool", [(w3, self.w_out[:, cg:cg + 256].rearrange("(kc p) n -> p kc n", p=128))], writes=[wr], key=wr.name)
            for m in range(2):
                c = cg // 128 + m
                xt, xr = xts[c % 2]
                hs, hr = hst[c % 2]
                self.load(xt, self.xT[c * 128:(c + 1) * 128, :], xr)
                for nh in range(2):
                    bk, br = self.bank()
                    for kc in range(32):
                        self.mm(bk[:], w3[:, kc, m * 128:(m + 1) * 128], mergedT[:, kc, nh * 512:(nh + 1) * 512], kc == 0, kc == 31, [wr, mr], [br])
                    self.tt("dve", hs[:, nh * 512:(nh + 1) * 512], bk[:], xt[:, nh * 512:(nh + 1) * 512], ALU.add, [br, xr], [hr])
                self.A(sq, hs, AF.Square, [hr], [sqr])
                self.tt("dve", ssqp, ssqp, sq, ALU.add, [sqr, ssr], [ssr])
                self.store(self.h1T_s[c * 128:(c + 1) * 128, :], hs, self.res(("h1", c)), hr)
        for nh in range(2):
            bk, br = self.bank()
            self.mm(bk[:], self.ones32, ssqp[:, nh * 512:(nh + 1) * 512], True, True, [cst, ssr], [br])
            self.A(sq[:, nh * 512:(nh + 1) * 512], bk[:], AF.Sqrt, [br, cst], [sqr], scale=1.0 / D, bias=self.epsc)
        P.op("dve", lambda e: e.reciprocal(out=rstd, in_=sq), [sqr], [rr])
        def ld4(c):
            hs, hr = hst[c % 2]
            self.load(hs, self.h1T_s[c * 128:(c + 1) * 128, :], hr, reads=[self.res(("h1", c))])
        ld4(0)
        ld4(1)
        for c in range(32):
            hs, hr = hst[c % 2]
            hb, hbr = h16[c % 2]
            self.stt(hb.bitcast(BF16), hs, self.gffn[:, c:c + 1], rstd, ALU.mult, ALU.mult, [hr, rr, cst], [hbr])
            if c + 2 < 32:
                ld4(c + 2)
            self.store(self.hn2T_s[c * 128:(c + 1) * 128, :], hb.bitcast(BF16), self.res("hn2T_s"), hbr)

    def phase5(self):
        P = self.P
        cst = self.res("const")
        hn2 = self.carve(16384, BF16, shape=(32, 1024))
        hr_ = self.res("hn2")
        qpT = self.carve(8192, BF16, shape=(16, 1024))
        qpr = self.res("qpT")
        wts = [(self.carve(2048, BF16, shape=(32, 128)), self.res(f"wt{i}")) for i in range(2)]
        self.load(hn2, self.hn2T_s.rearrange("(c p) t -> p c t", p=128), hr_, reads=[self.res("hn2T_s")])
        wi = 0
        NCHAIN = 4
        cb_ = []
        for k in range(NCHAIN):
            cb_.append(dict(v12=self.carve(32, shape=(2, 16)), tmp128=self.carve(128), cand=self.carve(256, shape=(16, 16)),
                            cand2=self.carve(256), c24=self.carve(24), e16=self.carve(16), r=self.res(f"s5_{k}")))
        sc = self.carve(8 * 8 * 16)
        scrs = [self.res(f"sc5_{k}") for k in range(NCHAIN)]
        Es = [(self.carve(256, BF16), self.res(f"E{i}")) for i in range(4)]
        Gps = [(self.carve(2048, BF16, shape=(8, 512)), self.res(f"Gp{i}")) for i in range(2)]
        GTb = self.carve(4096, shape=(4, 1024))
        gtr = self.res("GTb")
        gl, glr = self.carve(1024), self.res("gl0")
        ast = [(self.carve(512), self.res(f"ast{i}")) for i in range(2)]
        keys = self.keys16

        def chain(tt, h, B, bkbr):
            v12, tmp128, cand, cand2, c24, e16, s5 = B["v12"], B["tmp128"], B["cand"], B["cand2"], B["c24"], B["e16"], B["r"]
            scr_ = scrs[tt % NCHAIN]
            so = (tt * 8 + h) * 16
            bk, br = bkbr
            for half in range(2):
                sv = bk[:, half * 128:(half + 1) * 128]
                P.op("dve", lambda e, half=half, sv=sv: e.max(out=v12[:, half, 0:8], in_=sv), [br], [s5], hz=True)
                yield
                P.op("dve", lambda e, half=half, sv=sv: e.match_replace(out=tmp128, in_to_replace=v12[:, half, 0:8], in_values=sv, imm_value=-1e30), [br, s5], [s5], hz=True)
                yield
                P.op("dve", lambda e, half=half, sv=sv: e.max(out=v12[:, half, 8:16], in_=tmp128), [s5], [s5], hz=True)
                yield
            self.tt("dve", cand, v12[:, 0, :].unsqueeze(2).broadcast_to([128, 16, 16]),
                    v12[:, 1, :].unsqueeze(1).broadcast_to([128, 16, 16]), ALU.add, [s5], [s5])
            yield
            candf = cand.rearrange("p a b -> p (a b)")
            P.op("dve", lambda e: e.max(out=c24[:, 0:8], in_=candf), [s5], [s5], hz=True)
            yield
            P.op("dve", lambda e: e.match_replace(out=cand2, in_to_replace=c24[:, 0:8], in_values=candf, imm_value=-1e30), [s5], [s5], hz=True)
            yield
            P.op("dve", lambda e: e.max(out=c24[:, 8:16], in_=cand2), [s5], [s5], hz=True)
            yield
            P.op("dve", lambda e: e.match_replace(out=candf, in_to_replace=c24[:, 8:16], in_values=cand2, imm_value=-1e30), [s5], [s5], hz=True)
            yield
            P.op("dve", lambda e: e.max(out=c24[:, 16:24], in_=candf), [s5], [s5], hz=True)
            yield
            s_tau = sc[:, so + 0:so + 1]
            s_negm = sc[:, so + 1:so + 2]
            s_Z = sc[:, so + 2:so + 3]
            s_lnz = sc[:, so + 3:so + 4]
            s_shift = sc[:, so + 4:so + 5]
            self.tt("dve", s_tau, c24[:, 15:16], c24[:, 16:17], ALU.add, [s5], [scr_])
            yield
            self.ts("dve", s_tau, s_tau, 0.5, None, ALU.mult, None, [scr_], [scr_])
            yield
            self.ts("dve", s_negm, c24[:, 0:1], -1.0, None, ALU.mult, None, [s5], [scr_])
            yield
            self.A(e16, c24[:, 0:16], AF.Exp, [s5, scr_], [s5, scr_], bias=s_negm, accum=s_Z)
            yield
            self.A(s_lnz, s_Z, AF.Ln, [scr_], [scr_])
            yield
            self.tt("dve", s_shift, s_negm, s_lnz, ALU.subtract, [scr_], [scr_])
            yield

        for h in range(8):
            for c in (2 * h, 2 * h + 1):
                w3, wr = wts[wi % 2]
                wi += 1
                P.dma("pool", [(w3, self.w_q[:, c * 128:(c + 1) * 128].rearrange("(kc p) n -> p kc n", p=128))], writes=[wr], key=wr.name)
                for nh in range(2):
                    bk, br = self.bank()
                    for kc in range(32):
                        self.mm(bk[:], w3[:, kc, :], hn2[:, kc, nh * 512:(nh + 1) * 512], kc == 0, kc == 31, [wr, hr_], [br])
                    self.copy(self.eveng(), qpT[:, c, nh * 512:(nh + 1) * 512], bk[:], [br], [qpr])
            for t0 in range(0, 8, NCHAIN):
                gens = []
                for k in range(NCHAIN):
                    tt = t0 + k
                    tsl = slice(tt * 128, (tt + 1) * 128)
                    bk, br = self.bank()
                    for half in range(2):
                        self.mm(bk[:, half * 128:(half + 1) * 128], qpT[:, 2 * h + half, tsl], keys[:, 2 * h + half, :], True, True, [qpr, cst], [br])
                    gens.append(chain(tt, h, cb_[k], (bk, br)))
                while gens:
                    for g in list(gens):
                        try:
                            next(g)
                        except StopIteration:
                            gens.remove(g)
        GTbs = [(GTb, gtr), (self.carve(4096, shape=(4, 1024)), self.res("GTb1"))]

        bctr5 = {"s": 0, "u": 0, "t": 0}

        def pbank5(kind):
            lo, cnt = {"s": (0, 5), "u": (5, 2), "t": (7, 1)}[kind]
            b = self.banks[lo + bctr5[kind] % cnt]
            bctr5[kind] += 1
            return b

        def front(n):
            ig, tt = n // 8, n % 8
            tsl = slice(tt * 128, (tt + 1) * 128)
            Gp, Gpr = Gps[n % 2]
            for h in range(8):
                so = (tt * 8 + h) * 16
                scr_ = scrs[tt % NCHAIN]
                E, Er = Es[h % 4]
                bk, br = self.bank()
                o3 = bk[:].rearrange("p (i j) -> p i j", i=4)
                self.mm(o3, qpT[:, 2 * h, tsl], keys[:, 2 * h, ig * 4:(ig + 1) * 4].unsqueeze(2).broadcast_to([128, 4, 128]),
                        True, False, [qpr, cst], [br])
                self.mm(o3, qpT[:, 2 * h + 1, tsl], keys[:, 2 * h + 1, :].unsqueeze(1).broadcast_to([128, 4, 128]),
                        False, True, [qpr, cst], [br])
                self.A(E, bk[:], AF.Exp, [br, scr_], [Er], bias=sc[:, so + 4:so + 5])
                self.stt(Gp[:, h, :], bk[:], sc[:, so:so + 1], E, ALU.is_ge, ALU.mult, [br, Er], [Gpr], sreads=[scr_])
            self.tt("dve", Gp[:, 0:4, :], Gp[:, 0:4, :], Gp[:, 4:8, :], ALU.add, [Gpr], [Gpr])
            self.tt("dve", Gp[:, 0:2, :], Gp[:, 0:2, :], Gp[:, 2:4, :], ALU.add, [Gpr], [Gpr])
            self.tt("dve", Gp[:, 0, :], Gp[:, 0, :], Gp[:, 1, :], ALU.add, [Gpr], [Gpr])

        def back(n):
            ig, tt = n // 8, n % 8
            tsl = slice(tt * 128, (tt + 1) * 128)
            Gp, Gpr = Gps[n % 2]
            gb_, gbr_ = GTbs[ig % 2]
            bk, br = self.bank()
            for ii in range(4):
                self.mm(bk[:, ii * 128:(ii + 1) * 128], Gp[:, 0, ii * 128:(ii + 1) * 128], self.ident16, True, True, [Gpr, cst], [br])
            self.copy("act", gb_[:, :, tsl], bk[:].rearrange("p (i t) -> p i t", i=4), [br], [gbr_])

        ustate = {}

        def ugrp(ig, k):
            nonlocal wi
            q, nh = k // 2, k % 2
            i = ig * 4 + q
            gb_, gbr_ = GTbs[ig % 2]
            if nh == 0:
                w3, wr = wts[wi % 2]
                wi += 1
                P.dma("pool", [(w3, self.uT[:, i * 128:(i + 1) * 128].rearrange("(kc p) n -> p kc n", p=128))], writes=[wr], key=wr.name)
                ustate["w"] = (w3, wr)
            w3, wr = ustate["w"]
            bk, br = self.bank()
            for kc in range(32):
                self.mm(bk[:], w3[:, kc, :], hn2[:, kc, nh * 512:(nh + 1) * 512], kc == 0, kc == 31, [wr, hr_], [br])
            self.A(gl[:, nh * 512:(nh + 1) * 512], bk[:], AF.Gelu, [br], [glr])
            if nh == 1:
                a16, a16r = ast[i % 2]
                self.tt("dve", a16.bitcast(BF16), gl, gb_[:, q, :], ALU.mult, [glr, gbr_], [a16r])
                self.store(self.actT_s[i * 128:(i + 1) * 128, :], a16.bitcast(BF16), self.res(("act", i)), a16r)

        NS = 32 * 8
        for n in range(NS):
            front(n)
            if n >= 8:
                ugrp(n // 8 - 1, n % 8)
            if n >= 1:
                back(n - 1)
        back(NS - 1)
        for k in range(8):
            ugrp(31, k)

    def phase6(self):
        P = self.P
        cst = self.res("const")
        acts = [(self.carve(4096, BF16, shape=(8, 1024)), self.res(f"ac{i}")) for i in range(3)]
        vts = [(self.carve(2048, BF16, shape=(8, 512)), self.res(f"vw{i}")) for i in range(3)]
        hst = [(self.carve(1024), self.res(f"h6_{i}")) for i in range(2)]
        ost = [(self.carve(1024), self.res(f"o6_{i}")) for i in range(2)]
        sq = self.carve(1024)
        sqr = self.res("sq6")
        ssqp = self.carve(1024)
        ssr = self.res("ssq6")
        rstd = self.carve(1024)
        rr = self.res("rstd6")
        P.op("dve", lambda e: e.memset(ssqp, 0.0), [], [ssr])
        NRES = 5
        resid = [(self.carve(4096, BF16, shape=(8, 1024)), self.res(f"acres{i}")) for i in range(NRES)]
        li = 0
        for ps in range(8):
            col0 = ps * 512
            bks = [self.bank() for _ in range(8)]
            for eg in range(16):
                vw, vwr = vts[li % 3]
                e0 = eg * 8
                if eg < NRES:
                    ac, acr = resid[eg]
                else:
                    ac, acr = acts[li % 3]
                li += 1
                if eg >= NRES or ps == 0:
                    P.dma("sp", [(ac, self.actT_s[e0 * 128:(e0 + 8) * 128, :].rearrange("(e p) t -> p e t", p=128))],
                          reads=[self.res(("act", e0 + k)) for k in range(8)] if ps == 0 else (), writes=[acr], key=acr.name)
                P.dma("pool", [(vw, self.pv[e0 * 128:(e0 + 8) * 128, col0:col0 + 512].rearrange("(e p) n -> p e n", p=128))],
                      writes=[vwr], key=vwr.name)
                for k in range(8):
                    e = e0 + k
                    for cc in range(4):
                        for nh in range(2):
                            bk, br = bks[cc * 2 + nh]
                            self.mm(bk[:], vw[:, k, cc * 128:(cc + 1) * 128], ac[:, k, nh * 512:(nh + 1) * 512], e == 0, e == 127, [vwr, acr], [br])
            for cc in range(4):
                c = ps * 4 + cc
                hs, hr = hst[c % 2]
                os_, osr = ost[c % 2]
                self.load(hs, self.h1T_s[c * 128:(c + 1) * 128, :], hr)
                for nh in range(2):
                    bk, br = bks[cc * 2 + nh]
                    self.tt("dve", os_[:, nh * 512:(nh + 1) * 512], bk[:], hs[:, nh * 512:(nh + 1) * 512], ALU.add, [br, hr], [osr])
                self.A(sq, os_, AF.Square, [osr], [sqr])
                self.tt("dve", ssqp, ssqp, sq, ALU.add, [sqr, ssr], [ssr])
                self.store(self.h2T_s[c * 128:(c + 1) * 128, :], os_, self.res(("h2", c)), osr)
        for nh in range(2):
            bk, br = self.bank()
            self.mm(bk[:], self.ones32, ssqp[:, nh * 512:(nh + 1) * 512], True, True, [cst, ssr], [br])
            self.A(sq[:, nh * 512:(nh + 1) * 512], bk[:], AF.Sqrt, [br, cst], [sqr], scale=1.0 / D, bias=self.epsc)
        P.op("dve", lambda e: e.reciprocal(out=rstd, in_=sq), [sqr], [rr])
        def ld6(c):
            hs, hr = hst[c % 2]
            self.load(hs, self.h2T_s[c * 128:(c + 1) * 128, :], hr, reads=[self.res(("h2", c))])
        ld6(0)
        ld6(1)
        for c in range(32):
            hs, hr = hst[c % 2]
            os_, osr = ost[c % 2]
            self.stt(os_, hs, self.gfin[:, c:c + 1], rstd, ALU.mult, ALU.mult, [hr, rr, cst], [osr])
            if c + 2 < 32:
                ld6(c + 2)
            self.store(self.outT[c * 128:(c + 1) * 128, :], os_, self.res(("out", c)), osr)


def _t5_bucket(dist):
    dist = np.asarray(dist, dtype=np.int64)
    max_exact = 16
    d32 = np.maximum(dist, 1).astype(np.float32)
    val = np.log(d32 / np.float32(max_exact)) / np.float32(math.log(128 / max_exact)) * np.float32(32 - max_exact)
    large = max_exact + val.astype(np.int32)
    large = np.minimum(large, 31)
    return np.where(dist < max_exact, dist, large)


def _pcol(vec, C):
    return np.ascontiguousarray(np.asarray(vec, np.float32).reshape(C, 128).T)


def prepare_inputs(x, norm_mix, w_in, conv_w, conv_b, w_br_attn, w_br_conv, b_gate, rel_bias,
                   w_out, norm_ffn, peer_w_q, peer_sub_keys, peer_u, peer_v, norm_final):
    f = lambda a: np.ascontiguousarray(np.asarray(a, dtype=np.float32))
    x = f(x)
    rel_bias = f(rel_bias)
    shared = {
        "w_in": f(w_in[0]), "w_bra": f(w_br_attn[0]), "w_brc": f(w_br_conv[0]), "w_out": f(w_out[0]),
        "w_q": f(peer_w_q[0]), "uT": np.ascontiguousarray(f(peer_u[0]).T), "pv": f(peer_v[0]),
        "keysT": np.ascontiguousarray(np.transpose(f(peer_sub_keys[0]), (3, 0, 1, 2)).reshape(128, 16 * 128)),
        "gmixb": f(norm_mix[0]).reshape(1, D),
        "identd": np.eye(128, dtype=np.float32),
    }
    qi = np.arange(128)[:, None]
    kj = np.arange(512)[None, :]
    nearb = np.zeros((2, 16, 128, 512), np.float32)
    cmask = np.zeros((128, 2, 512), np.float32)
    for par in range(2):
        dist = (256 + par * 128 + qi) - kj
        bk = _t5_bucket(np.maximum(dist, 0))
        nearb[par] = np.transpose(rel_bias[bk], (2, 0, 1))
        cmask[:, par, :] = np.where(dist >= 0, 0.0, BIGNEG)
    shared["nearb"] = np.ascontiguousarray(nearb.reshape(2 * 16 * 128, 512))
    cw = f(conv_w[0])[:, 0, :]
    convw = np.ascontiguousarray(np.transpose(cw.reshape(3, 16, 128), (2, 1, 0))).reshape(128, 48)
    bg = np.ascontiguousarray(np.transpose(f(b_gate[0]).reshape(2, 32, 128), (2, 0, 1))).reshape(128, 64)
    in_maps = []
    for c in range(8):
        b, half = c // 2, c % 2
        own = x[b, half * 1024:(half + 1) * 1024]
        prev = x[b, 0:1024] if half == 1 else np.zeros_like(own)
        pastb = np.zeros((128, 8, 8), np.float32)
        valid = np.zeros((128, 8, 8), np.float32)
        ownm = np.zeros((128, 8, 8), np.float32)
        for i in range(8):
            cb = 4 + i // 2
            for n in range(8):
                ok = (n < cb) and (half == 1 or n >= 4)
                pastb[:, i, n] = 0.0 if ok else -1e30
                valid[:, i, n] = 1.0 if ok else 0.0
                ownm[:, i, n] = 1.0 if n == cb else 0.0
        small = np.concatenate([
            _pcol(norm_mix[0], 32), _pcol(norm_ffn[0], 32), _pcol(norm_final, 32),
            convw, _pcol(conv_b[0], 16), bg,
            np.tile(rel_bias[31][None, :], (128, 1)),
            pastb.reshape(128, 64), valid.reshape(128, 64), ownm.reshape(128, 64),
            cmask.reshape(128, 1024), np.ones((128, 128), np.float32), np.full((128, 1), EPS, np.float32),
        ], axis=1)
        pad = np.zeros((128, 1600 - small.shape[1]), np.float32)
        m = dict(shared)
        m["xctx"] = np.ascontiguousarray(np.concatenate([prev, own], axis=0))
        m["xT"] = np.ascontiguousarray(own.T)
        m["smallc"] = np.ascontiguousarray(np.concatenate([small, pad], axis=1))
        in_maps.append(m)
    return in_maps


def kernel(**inputs):
    in_maps = prepare_inputs(**inputs)
    nc = Builder().build()
    res = run_bass_kernel_spmd(nc, in_maps, core_ids=list(range(8)))
    out = np.empty((4, 2048, D), np.float32)
    for c in range(8):
        b, half = c // 2, c % 2
        out[b, half * 1024:(half + 1) * 1024, :] = res.results[c]["outT"].T
    return out
```

```python
import math
from contextlib import ExitStack
import numpy as np
import concourse.bass as bass
import concourse.mybir as mybir
from concourse.bass_utils import run_bass_kernel_spmd

F32 = mybir.dt.float32
BF16 = mybir.dt.bfloat16
AF = mybir.ActivationFunctionType
ALU = mybir.AluOpType
AX = mybir.AxisListType

SEM_LIMIT = 24000
ENGS = ("pe", "act", "dve", "pool", "sp")


class Res:
    __slots__ = ("name", "w", "r", "hz")

    def __init__(self, name):
        self.name = name
        self.w = None
        self.r = {}
        self.hz = False


class Prog:
    def __init__(self, nc, stack, n_sems=100):
        self.nc = nc
        self.sems = [stack.enter_context(nc.semaphore(f"s{i}")) for i in range(n_sems)]
        self.next_sem = 0
        self.q = {e: [] for e in ENGS}
        self.esem = {}
        self.ecnt = {}
        self.allsem = {}
        for e in ENGS:
            self._new_esem(e)
        self.waited = {e: {} for e in ENGS}
        self.ksem = {}
        self.free_ks = []

    def _alloc_sem(self):
        i = self.next_sem
        self.next_sem += 1
        assert i < len(self.sems), "out of semaphores"
        return i

    def _new_esem(self, e):
        self.esem[e] = self._alloc_sem()
        self.ecnt[e] = 0

    def _deps(self, eng, reads, writes, skip_own=False, sreads=()):
        need = {}
        own = self.esem[eng]
        strict_own = 0

        def add(s, v):
            if need.get(s, 0) < v:
                need[s] = v
        for r in reads:
            if r.w is not None:
                add(*r.w)
                if r.w[0] == own:
                    strict_own = max(strict_own, r.w[1])
        for r in sreads:
            if r.w is not None:
                add(*r.w)
                if r.w[0] == own:
                    strict_own = max(strict_own, r.w[1])
        for w in writes:
            if w.w is not None:
                add(*w.w)
            for s, v in w.r.items():
                add(s, v)
        if own in need:
            if skip_own or strict_own == 0:
                del need[own]
            else:
                need[own] = strict_own
        return self._filter(eng, need, False)

    def _filter(self, eng, need, skip_own=False):
        out = []
        wd = self.waited[eng]
        for s, v in need.items():
            if skip_own and s == self.esem[eng]:
                continue
            if wd.get(s, 0) >= v:
                continue
            wd[s] = v
            out.append((s, v))
        return out

    def _commit(self, ev, reads, writes):
        s, v = ev
        self.allsem[s] = max(self.allsem.get(s, 0), v)
        for r in reads:
            if r.r.get(s, 0) < v:
                r.r[s] = v
        for w in writes:
            w.w = ev
            w.r = {}

    def op(self, eng, fn, reads=(), writes=(), sreads=(), hz=False):
        if self.ecnt[eng] >= SEM_LIMIT:
            self._new_esem(eng)
        waits = self._deps(eng, reads, writes, skip_own=(eng == "pe"), sreads=sreads)
        self.ecnt[eng] += 1
        ev = (self.esem[eng], self.ecnt[eng])
        self.q[eng].append((waits, fn, ev[0], 1))
        self._commit(ev, list(reads) + list(sreads), writes)
        if hz:
            for w in writes:
                w.hz = True
        return ev

    def dma(self, queue, pieces, reads=(), writes=(), key=None):
        ks = self.ksem.get(key)
        if ks is None or ks[1] + 16 * len(pieces) >= SEM_LIMIT:
            while self.free_ks and self.free_ks[-1][1] + 16 * len(pieces) >= SEM_LIMIT:
                self.free_ks.pop()
            ks = self.free_ks.pop() if self.free_ks else [self._alloc_sem(), 0]
            self.ksem[key] = ks
        waits = self._deps(queue, reads, writes)
        for i, (o, a) in enumerate(pieces):
            fn = (lambda e, o=o, a=a: e.dma_start(out=o, in_=a))
            self.q[queue].append((waits if i == 0 else [], fn, ks[0], 16))
        ks[1] += 16 * len(pieces)
        ev = (ks[0], ks[1])
        self._commit(ev, reads, writes)
        return ev

    def barrier(self):
        for e in ENGS:
            need = dict(self.allsem)
            waits = self._filter(e, need, skip_own=True)
            if waits:
                self.q[e].append((waits, None, None, 0))
        self.free_ks.extend(self.ksem.values())
        self.ksem = {}

    def emit(self):
        nc = self.nc
        sems = self.sems
        engmap = {"pe": "tensor", "act": "scalar", "dve": "vector", "pool": "gpsimd", "sp": "sync"}
        with nc.Block() as block:
            for e in ENGS:
                ops = self.q[e]
                if not ops:
                    continue

                def body(eng, ops=ops):
                    for waits, fn, s, inc in ops:
                        for (ws, wv) in waits:
                            eng.wait_ge(sems[ws], wv)
                        if fn is not None:
                            fn(eng).then_inc(sems[s], inc)
                getattr(block, engmap[e])(body)


D = 4096
T = 1024
NCH = 32
SCALE = 128 ** -0.5
EPS = 1e-6
BIGNEG = -30000.0
W_IN = 20480


class Builder:
    def __init__(self, stage=99, debug=False):
        self.stage = stage
        self.debug = debug
        self.nc = bass.Bass("TRN2", target_bir_lowering=False)
        self.st = ExitStack()
        self.R = {}
        self.bi = 0
        self.evi = 0

    def res(self, key):
        r = self.R.get(key)
        if r is None:
            r = Res(str(key))
            self.R[key] = r
        return r

    def din(self, name, shape):
        return self.nc.dram_tensor(name, list(shape), F32, kind="ExternalInput").ap()

    def dscr(self, name, shape, dt):
        kind = "ExternalOutput" if (self.debug and name in self.debug) else "Internal"
        return self.nc.dram_tensor(name, list(shape), dt, kind=kind).ap()

    def carve(self, nwords, dt=F32, shape=None):
        v = self.arena[:, self.off:self.off + nwords]
        self.off += nwords
        assert self.off <= self.arena_words, f"arena overflow {self.off}"
        if dt != F32:
            v = v.bitcast(dt)
        if shape is not None:
            if len(shape) == 2:
                v = v.rearrange("p (a b) -> p a b", a=shape[0])
            elif len(shape) == 3:
                v = v.rearrange("p (a b c) -> p a b c", a=shape[0], b=shape[1])
        return v

    def bank(self):
        b = self.banks[self.bi % 8]
        self.bi += 1
        return b

    def eveng(self):
        self.evi += 1
        return "act" if self.evi % 2 == 0 else "dve"

    def mm(self, out, lhsT, rhs, start, stop, reads, writes):
        self.P.op("pe", lambda e: e.matmul(out, lhsT, rhs, start=start, stop=stop), reads, writes)

    def tr(self, out, in_, reads, writes):
        idn = self.ident16
        self.P.op("pe", lambda e: e.transpose(out, in_, idn), list(reads) + [self.res("const")], writes)

    def A(self, out, in_, func, reads, writes, bias=None, scale=None, accum=None):
        kw = {}
        if bias is not None:
            kw["bias"] = bias
        if scale is not None:
            kw["scale"] = scale
        if accum is not None:
            kw["accum_out"] = accum
        self.P.op("act", lambda e: e.activation(out=out, in_=in_, func=func, **kw), reads, writes, hz=(accum is not None))

    def copy(self, eng, out, in_, reads, writes):
        if eng == "act":
            self.A(out, in_, AF.Copy, reads, writes)
        else:
            self.P.op(eng, lambda e: e.tensor_copy(out=out, in_=in_), reads, writes)

    def tt(self, eng, out, in0, in1, op, reads, writes):
        self.P.op(eng, lambda e: e.tensor_tensor(out=out, in0=in0, in1=in1, op=op), reads, writes)

    def ts(self, eng, out, in0, s1, s2, op0, op1, reads, writes, sreads=()):
        if op1 is None:
            self.P.op(eng, lambda e: e.tensor_scalar(out=out, in0=in0, scalar1=s1, scalar2=None, op0=op0), reads, writes, sreads=sreads)
        else:
            self.P.op(eng, lambda e: e.tensor_scalar(out=out, in0=in0, scalar1=s1, scalar2=s2, op0=op0, op1=op1), reads, writes, sreads=sreads)

    def stt(self, out, in0, scalar, in1, op0, op1, reads, writes, sreads=()):
        self.P.op("dve", lambda e: e.scalar_tensor_tensor(out=out, in0=in0, scalar=scalar, in1=in1, op0=op0, op1=op1),
                  reads, writes, sreads=sreads)

    def red(self, out, in_, op, reads, writes):
        self.P.op("dve", lambda e: e.tensor_reduce(out=out, in_=in_, axis=AX.X, op=op), reads, writes)

    def load(self, out, in_, res, reads=(), queue="sp"):
        self.P.dma(queue, [(out, in_)], reads=reads, writes=[res], key=res.name)

    def store(self, out, in_, dres, sres):
        self.P.dma("sp", [(out, in_)], reads=[sres], writes=[dres], key="st_" + sres.name)

    def build(self):
        nc = self.nc
        st = self.st
        with st:
            self.P = Prog(nc, st)
            self.arena_words = 52224
            self.arena = st.enter_context(nc.sbuf_tensor("arena", [128, self.arena_words], F32))
            self.banks = []
            for i in range(8):
                t = st.enter_context(nc.psum_tensor(f"bank{i}", [128, 512], F32))
                self.banks.append((t, Res(f"bank{i}")))
            self.off = 0
            self._declare_io()
            self._consts()
            self.base = self.off
            stages = [self.phase12, self.phase3, self.phase4, self.phase5, self.phase6]
            for i, f in enumerate(stages):
                if self.stage > i:
                    self.off = self.base
                    f()
                    self.P.barrier()
            self.P.barrier()
            self.P.emit()
        return nc

    def _declare_io(self):
        d = self.din
        self.xctx = d("xctx", [2048, D])
        self.xT = d("xT", [D, T])
        self.w_in = d("w_in", [D, W_IN])
        self.w_bra = d("w_bra", [2048, D])
        self.w_brc = d("w_brc", [2048, D])
        self.w_out = d("w_out", [D, D])
        self.w_q = d("w_q", [D, 2048])
        self.uT = d("uT", [D, 16384])
        self.pv = d("pv", [16384, D])
        self.keysT = d("keysT", [128, 16 * 128])
        self.nearb = d("nearb", [2 * 16 * 128, 512])
        self.smallc = d("smallc", [128, 1600])
        self.gmixb = d("gmixb", [1, D])
        self.identd = d("identd", [128, 128])
        s = self.dscr
        self.qT_s = s("qT_s", [2048, T], BF16)
        self.kT_s = s("kT_s", [2048, 2048], BF16)
        self.v_s = s("v_s", [2048, 2048], BF16)
        self.convT_s = s("convT_s", [2048, T], BF16)
        self.gA_s = s("gA_s", [D, T], F32)
        self.gC_s = s("gC_s", [D, T], F32)
        self.attnT_s = s("attnT_s", [2048, T], BF16)
        self.h1T_s = s("h1T_s", [D, T], F32)
        self.hn2T_s = s("hn2T_s", [D, T], BF16)
        self.GT_s = s("GT_s", [16384, T], F32)
        self.actT_s = s("actT_s", [16384, T], BF16)
        self.h2T_s = s("h2T_s", [D, T], F32)
        self.outT = nc_out = self.nc.dram_tensor("outT", [D, T], F32, kind="ExternalOutput").ap()

    def _consts(self):
        P = self.P
        c = self.res("const")
        self.ident16 = self.carve(64, BF16)
        self.keys16 = self.carve(1024, BF16, shape=(16, 128))
        self.small = self.carve(1600)
        sm = self.small
        o = 0

        def take(n, shape=None):
            nonlocal o
            v = sm[:, o:o + n]
            o += n
            if shape is not None:
                if len(shape) == 2:
                    v = v.rearrange("p (a b) -> p a b", a=shape[0])
            return v
        self.gmix = take(32)
        self.gffn = take(32)
        self.gfin = take(32)
        self.convw = take(48, (16, 3))
        self.convb = take(16)
        self.bgate = take(64, (2, 32))
        self.cfar = take(16)
        self.pastb = take(64, (8, 8))
        self.valid01 = take(64, (8, 8))
        self.own01 = take(64, (8, 8))
        self.cmask = take(1024, (2, 512))
        self.ones32 = take(128)
        self.epsc = take(1)
        P.dma("pool", [(self.ident16, self.identd)], writes=[c], key="c_id")
        P.dma("pool", [(self.keys16, self.keysT.rearrange("p (a b) -> p a b", a=16))], writes=[c], key="c_keys")
        P.dma("sp", [(self.small, self.smallc)], writes=[c], key="c_small")
        self.misc = self.carve(512)
        P.barrier()

    def phase12(self):
        P = self.P
        hnT = self.carve(16384, BF16, shape=(32, 1024))
        hnr = self.res("hnT")
        hn_tail = self.carve(32, BF16, shape=(32, 2))
        tailr = self.res("hn_tail")
        mark = self.off
        xts = [(self.carve(4096), self.res(f"xt{i}")) for i in range(2)]
        gb = self.carve(4096)
        gbr = self.res("gb")
        xs = self.carve(2048, BF16)
        xsr = self.res("xs")
        stat = self.misc
        ssq, std, rstd = stat[:, 0:1], stat[:, 1:2], stat[:, 2:3]
        sr = self.res("stat")
        self.load(gb, self.gmixb.broadcast_to([128, D]), gbr)
        norm_end = self.off
        self.off = mark
        wts = [(self.carve(4096, BF16), self.res(f"wt{i}")) for i in range(2)]
        stg = [(self.carve(1024), self.res(f"stg{i}")) for i in range(3)]
        ccs = self.carve(1032)
        gated = self.carve(1032)
        yb = self.carve(1024)
        ccr, gr, yr = self.res("ccs"), self.res("gated"), self.res("yb")
        self.off = max(self.off, norm_end)
        self.wts = wts
        self.wi = 0
        self.si = 0
        cst = self.res("const")

        def norm_seg(seg):
            for tt in range(8):
                xt, xr = xts[tt % 2]
                self.load(xt, self.xctx[seg * 1024 + tt * 128: seg * 1024 + (tt + 1) * 128, :], xr)
                self.A(xs, xt, AF.Square, [xr], [xsr, sr], accum=ssq)
                self.A(std, ssq, AF.Sqrt, [sr, cst], [sr], scale=1.0 / D, bias=self.epsc)
                P.op("dve", lambda e: e.reciprocal(out=rstd, in_=std), [sr], [sr])
                self.stt(xs, xt, rstd, gb, ALU.mult, ALU.mult, [xr, gbr], [xsr], sreads=[sr])
                for cg in range(4):
                    bk, br = self.bank()
                    bk16 = bk[:].bitcast(BF16)
                    for j in range(8):
                        cc = cg * 8 + j
                        self.tr(bk16[:, j * 128:(j + 1) * 128], xs[:, cc * 128:(cc + 1) * 128], [xsr], [br])
                    self.copy(self.eveng(), hnT[:, cg * 8:(cg + 1) * 8, tt * 128:(tt + 1) * 128],
                              bk16.rearrange("p (c t) -> p c t", c=8), [br], [hnr])

        def wtile(KC, ncols):
            wt, wr = wts[self.wi % 2]
            self.wi += 1
            return wt[:, 0:KC * ncols].rearrange("p (k n) -> p k n", k=KC), wr
        self.wtile = wtile

        def stage_slot():
            s = stg[self.si % 3]
            self.si += 1
            return s

        def proj_fm(W, c0, ncols, epi, wcols=256, act=hnT, actr=hnr, KC=32, NT=1024):
            for g0 in range(c0, c0 + ncols, wcols):
                wt, wr = wtile(KC, wcols)
                P.dma("pool", [(wt, W[:, g0:g0 + wcols].rearrange("(kc p) n -> p kc n", p=128))], writes=[wr], key=wr.name)
                for m in range(wcols // 128):
                    for nh in range(NT // 512):
                        bk, br = self.bank()
                        for kc in range(KC):
                            self.mm(bk[:], wt[:, kc, m * 128:(m + 1) * 128], act[:, kc, nh * 512:(nh + 1) * 512],
                                    kc == 0, kc == KC - 1, [wr, actr], [br])
                        epi(g0 + m * 128, nh, bk, br)
        self.proj_fm = proj_fm
        self.stage_slot = stage_slot

        km32 = self.misc[:, 16:144].rearrange("p (h n) -> p h n", h=16)
        kmr = self.res("km")

        def epi_store_bf16(dst, col_base, tok0, kmeans=False):
            state = {}

            def epi(c0, nh, bk, br):
                if nh == 0:
                    state["slot"] = stage_slot()
                sl, slr = state["slot"]
                s16 = sl.bitcast(BF16)
                self.copy(self.eveng(), s16[:, nh * 512:(nh + 1) * 512], bk[:], [br], [slr])
                if nh == 1:
                    r0 = c0 - col_base
                    if kmeans:
                        hh = r0 // 128
                        b0 = tok0 // 256
                        self.red(km32[:, hh, b0:b0 + 4], s16[:, 0:1024].rearrange("p (n l) -> p n l", n=4), ALU.add, [slr], [kmr])
                    self.store(dst[r0:r0 + 128, tok0:tok0 + 1024], s16[:, 0:1024], self.res(("d", id(dst), r0)), slr)
            return epi

        def proj_v(seg):
            for g0 in range(4096, 6144, 256):
                wt, wr = wtile(32, 256)
                P.dma("pool", [(wt, self.w_in[:, g0:g0 + 256].rearrange("(kc p) n -> p kc n", p=128))], writes=[wr], key=wr.name)
                sl, slr = stage_slot()
                s16 = sl.bitcast(BF16).rearrange("p (t n) -> p t n", t=8)
                for tt in range(8):
                    bk, br = self.bank()
                    for kc in range(32):
                        self.mm(bk[:, 0:256], hnT[:, kc, tt * 128:(tt + 1) * 128], wt[:, kc, :], kc == 0, kc == 31, [wr, hnr], [br])
                    self.copy(self.eveng(), s16[:, tt, :], bk[:, 0:256], [br], [slr])
                self.store(self.v_s[seg * 1024:(seg + 1) * 1024, g0 - 4096:g0 - 4096 + 256].rearrange("(t p) n -> p t n", p=128),
                           s16, self.res(("v", seg, g0)), slr)

        norm_seg(0)
        self.copy("dve", hn_tail, hnT[:, :, 1022:1024], [hnr], [tailr])
        P.barrier()
        proj_fm(self.w_in, 2048, 2048, epi_store_bf16(self.kT_s, 2048, 0, kmeans=True))
        proj_v(0)
        P.barrier()
        self.load(gb, self.gmixb.broadcast_to([128, D]), gbr)
        norm_seg(1)
        P.barrier()
        proj_fm(self.w_in, 2048, 2048, epi_store_bf16(self.kT_s, 2048, 1024, kmeans=True))
        proj_v(1)
        proj_fm(self.w_in, 0, 2048, epi_store_bf16(self.qT_s, 0, 0))
        for ch in range(16):
            pcs = {}
            for nm, cbase in (("cc", 8192), ("cu", 10240), ("cb", 6144)):
                wt, wr = wtile(32, 128)
                g0 = cbase + ch * 128
                P.dma("pool", [(wt, self.w_in[:, g0:g0 + 128].rearrange("(kc p) n -> p kc n", p=128))], writes=[wr], key=wr.name)
                for nh in range(2):
                    bk, br = self.bank()
                    for kc in range(32):
                        self.mm(bk[:], wt[:, kc, :], hnT[:, kc, nh * 512:(nh + 1) * 512], kc == 0, kc == 31, [wr, hnr], [br])
                    pcs[(nm, nh)] = (bk, br)
                if nm != "cb":
                    bk, br = self.bank()
                    for kc in range(32):
                        self.mm(bk[:, 0:2], wt[:, kc, :], hn_tail[:, kc, :], kc == 0, kc == 31, [wr, tailr], [br])
                    pcs[(nm, 2)] = (bk, br)
                if nm == "cc":
                    for nh in range(2):
                        bk, br = pcs[("cc", nh)]
                        self.copy("act", ccs[:, 2 + nh * 512: 2 + (nh + 1) * 512], bk[:], [br], [ccr])
                    bk, br = pcs[("cc", 2)]
                    self.copy("act", ccs[:, 0:2], bk[:, 0:2], [br], [ccr])
                if nm == "cu":
                    for nh in range(2):
                        bk, br = pcs[("cu", nh)]
                        self.tt("dve", gated[:, 2 + nh * 512: 2 + (nh + 1) * 512], ccs[:, 2 + nh * 512: 2 + (nh + 1) * 512], bk[:],
                                ALU.mult, [br, ccr], [gr])
                    bk, br = pcs[("cu", 2)]
                    self.tt("dve", gated[:, 0:2], ccs[:, 0:2], bk[:, 0:2], ALU.mult, [br, ccr], [gr])
                    self.ts("dve", yb, gated[:, 2:1026], self.convw[:, ch, 2:3], self.convb[:, ch:ch + 1], ALU.mult, ALU.add,
                            [gr, cst], [yr])
                    self.stt(yb, gated[:, 1:1025], self.convw[:, ch, 1:2], yb, ALU.mult, ALU.add, [gr, cst, yr], [yr])
                    self.stt(yb, gated[:, 0:1024], self.convw[:, ch, 0:1], yb, ALU.mult, ALU.add, [gr, cst, yr], [yr])
                if nm == "cb":
                    sl, slr = stage_slot()
                    s16 = sl.bitcast(BF16)
                    for nh in range(2):
                        bk, br = pcs[("cb", nh)]
                        self.tt("dve", s16[:, nh * 512:(nh + 1) * 512], yb[:, nh * 512:(nh + 1) * 512], bk[:], ALU.mult,
                                [br, yr], [slr])
                    self.store(self.convT_s[ch * 128:(ch + 1) * 128, :], s16[:, 0:1024], self.res(("conv", ch)), slr)
        for which, dst in ((0, self.gA_s), (1, self.gC_s)):
            state = {}

            def epi(c0, nh, bk, br, which=which, dst=dst, state=state):
                if nh == 0:
                    state["slot"] = stage_slot()
                sl, slr = state["slot"]
                cidx = (c0 - 12288 - which * 4096) // 128
                self.A(sl[:, nh * 512:(nh + 1) * 512], bk[:], AF.Sigmoid, [br, cst], [slr], bias=self.bgate[:, which, cidx:cidx + 1])
                if nh == 1:
                    self.store(dst[cidx * 128:(cidx + 1) * 128, :], sl, self.res(("g", which, cidx)), slr)
            proj_fm(self.w_in, 12288 + which * 4096, 4096, epi)

    def phase3(self):
        P = self.P
        cst = self.res("const")
        qT = self.carve(8192, BF16, shape=(16, 1024))
        qr = self.res("qT")
        attnT = self.carve(8192, BF16, shape=(16, 1024))
        ar = self.res("attnT")
        negb = self.carve(1024, shape=(8, 16, 8))
        nbr = self.res("negb")
        km32 = self.misc[:, 16:144].rearrange("p (h n) -> p h n", h=16)
        km16 = self.carve(64, BF16, shape=(16, 8))
        kmr = self.res("km")
        kts = [(self.carve(1024, BF16), self.res(f"kt{i}")) for i in range(2)]
        vts = [(self.carve(1024, BF16, shape=(16, 128)), self.res(f"vt{i}")) for i in range(2)]
        nbs = [(self.carve(1024, shape=(2, 512)), self.res(f"nb{i}")) for i in range(2)]
        Ls = [(self.carve(2048), self.res(f"L{i}")) for i in range(3)]
        Ps = [(self.carve(1024, BF16), self.res(f"P{i}")) for i in range(3)]
        PTs = [(self.carve(1024, BF16, shape=(16, 128)), self.res(f"PT{i}")) for i in range(3)]
        gm = [self.carve(128, shape=(16, 8)) for _ in range(4)]
        gmr = self.res("gm")
        mx = self.carve(16)
        rs_all = self.carve(24)
        rsrs = [self.res(f"rs{i}") for i in range(3)]
        self.load(qT, self.qT_s.rearrange("(h p) t -> p h t", p=128), qr, reads=[self.res(("d", id(self.qT_s), r0)) for r0 in range(0, 2048, 128)])
        self.ts("dve", km16, km32, 1.0 / 256.0, None, ALU.mult, None, [kmr], [kmr])
        for i in range(8):
            bk, br = self.bank()
            for h in range(16):
                self.mm(bk[:, h * 8:(h + 1) * 8], qT[:, h, i * 128:(i + 1) * 128], km16[:, h, :], True, True, [qr, kmr], [br])
            pg = bk[:, 0:128].rearrange("p (h n) -> p h n", h=16)

            def bc(v):
                return v.unsqueeze(1).broadcast_to([128, 16, 8])

            def bm(v):
                return v.unsqueeze(2).broadcast_to([128, 16, 8])
            g0, g1, g2, eq = gm
            self.tt("dve", g0, pg, bc(self.pastb[:, i, :]), ALU.add, [br, cst], [gmr])
            self.red(mx, g0, ALU.max, [gmr], [gmr])
            self.tt("dve", eq, g0, bm(mx), ALU.is_equal, [gmr], [gmr])
            self.stt(g1, eq, -3e30, g0, ALU.mult, ALU.add, [gmr], [gmr])
            self.red(mx, g1, ALU.max, [gmr], [gmr])
            self.tt("dve", eq, g1, bm(mx), ALU.is_equal, [gmr], [gmr])
            self.stt(g2, eq, -3e30, g1, ALU.mult, ALU.add, [gmr], [gmr])
            self.red(mx, g2, ALU.max, [gmr], [gmr])
            self.tt("dve", eq, g0, bm(mx), ALU.is_ge, [gmr], [gmr])
            self.tt("dve", eq, eq, bc(self.valid01[:, i, :]), ALU.mult, [gmr, cst], [gmr])
            self.tt("dve", eq, eq, bc(self.own01[:, i, :]), ALU.add, [gmr, cst], [gmr])
            self.ts("dve", negb[:, i, :, :], eq, -1.0, -BIGNEG, ALU.add, ALU.mult, [gmr], [nbr])
        items = [(h, i) for h in range(16) for i in range(8)]
        bctr = {"s": 0, "t": 0, "v": 0}

        def pbank(kind):
            lo, n = {"s": (0, 4), "t": (4, 2), "v": (6, 2)}[kind]
            b = self.banks[lo + bctr[kind] % n]
            bctr[kind] += 1
            return b

        def stageA(idx):
            h, i = items[idx]
            kt, kr = kts[h % 2]
            nb, nr = nbs[h % 2]
            if i == 0:
                vt, vr = vts[h % 2]
                self.load(kt, self.kT_s[h * 128:(h + 1) * 128, :], kr)
                self.load(vt, self.v_s[:, h * 128:(h + 1) * 128].rearrange("(t p) d -> p t d", p=128), vr)
                self.load(nb, self.nearb.rearrange("(a h p) k -> p a h k", a=2, h=16)[:, :, h, :], nr)
                self.tt("dve", nb, nb, self.cmask, ALU.add, [nr, cst], [nr])
            cb = 4 + i // 2
            par = i % 2
            nkeys = (cb + 1) * 256
            L, Lr = Ls[idx % 3]
            Pb, Pr = Ps[idx % 3]
            rs = rs_all[:, (idx % 3) * 8:(idx % 3) * 8 + 8]
            rsr = rsrs[idx % 3]
            nbk = (nkeys + 511) // 512
            for b in range(nbk):
                w = min(512, nkeys - b * 512)
                bk, br = pbank("s")
                self.mm(bk[:, 0:w], qT[:, h, i * 128:(i + 1) * 128], kt[:, b * 512:b * 512 + w], True, True, [qr, kr], [br])
                nblk = w // 256
                self.stt(L[:, b * 512:b * 512 + w].rearrange("p (n l) -> p n l", n=nblk),
                         bk[:, 0:w].rearrange("p (n l) -> p n l", n=nblk), SCALE,
                         negb[:, i, h, 2 * b:2 * b + nblk].unsqueeze(2).broadcast_to([128, nblk, 256]),
                         ALU.mult, ALU.add, [br, nbr], [Lr])
            n0 = (cb - 1) * 256
            self.tt("dve", L[:, n0:n0 + 512], L[:, n0:n0 + 512], nb[:, par, :], ALU.add, [Lr, nr], [Lr])
            self.A(Pb[:, 0:n0], L[:, 0:n0], AF.Exp, [Lr, cst], [Pr, rsr], bias=self.cfar[:, h:h + 1], accum=rs[:, 0:1])
            self.A(Pb[:, n0:n0 + 512], L[:, n0:n0 + 512], AF.Exp, [Lr], [Pr, rsr], accum=rs[:, 1:2])

        def stageA2(idx):
            h, i = items[idx]
            nkeys = (4 + i // 2 + 1) * 256
            Pb, Pr = Ps[idx % 3]
            rs = rs_all[:, (idx % 3) * 8:(idx % 3) * 8 + 8]
            rsr = rsrs[idx % 3]
            self.tt("dve", rs[:, 2:3], rs[:, 0:1], rs[:, 1:2], ALU.add, [rsr], [rsr])
            P.op("dve", lambda e, rs=rs: e.reciprocal(out=rs[:, 3:4], in_=rs[:, 2:3]), [rsr], [rsr])
            self.ts("dve", Pb[:, 0:nkeys], Pb[:, 0:nkeys], rs[:, 3:4], None, ALU.mult, None, [Pr], [Pr], sreads=[rsr])

        def stageBT(idx):
            h, i = items[idx]
            cb = 4 + i // 2
            nkeys = (cb + 1) * 256
            Pb, Pr = Ps[idx % 3]
            PT, PTr = PTs[idx % 3]
            nchk = nkeys // 128
            for g in range((nchk + 7) // 8):
                n = min(8, nchk - g * 8)
                bk, br = pbank("t")
                bk16 = bk[:].bitcast(BF16)
                for j in range(n):
                    kc = g * 8 + j
                    self.tr(bk16[:, j * 128:(j + 1) * 128], Pb[:, kc * 128:(kc + 1) * 128], [Pr], [br])
                self.copy("act", PT[:, g * 8:g * 8 + n, :],
                          bk16[:, 0:n * 128].rearrange("p (c t) -> p c t", c=n), [br], [PTr])

        def stageBV(idx):
            h, i = items[idx]
            vt, vr = vts[h % 2]
            cb = 4 + i // 2
            nkeys = (cb + 1) * 256
            PT, PTr = PTs[idx % 3]
            nchk = nkeys // 128
            bk, br = pbank("v")
            for kc in range(nchk):
                self.mm(bk[:, 0:128], vt[:, kc, :], PT[:, kc, :], kc == 0, kc == nchk - 1, [vr, PTr], [br])
            self.copy("act", attnT[:, h, i * 128:(i + 1) * 128], bk[:, 0:128], [br], [ar])

        NI = len(items)
        stageA(0)
        stageA(1)
        stageA2(0)
        for idx in range(NI):
            stageBT(idx)
            if idx + 2 < NI:
                stageA(idx + 2)
            if idx + 1 < NI:
                stageA2(idx + 1)
            stageBV(idx)
        self.store(self.attnT_s.rearrange("(h p) t -> p h t", p=128), attnT, self.res("attnT_s"), ar)

    def phase4(self):
        P = self.P
        cst = self.res("const")
        mergedT = self.carve(16384, BF16, shape=(32, 1024))
        mr = self.res("mergedT")
        wbuf = self.carve(8192)
        ssqp = self.carve(1024)
        ssr = self.res("ssqp")
        mark = self.off
        A0 = self.carve(16384, BF16)
        attnT = A0[:, 0:16384].rearrange("p (h t) -> p h t", h=16)
        convT = A0[:, 16384:32768].rearrange("p (h t) -> p h t", h=16)
        a0r = self.res("A0")
        wbr = [(wbuf[:, i * 2048:(i + 1) * 2048].bitcast(BF16).rearrange("p (k n) -> p k n", k=16), self.res(f"wb{i}")) for i in range(4)]
        gts = [(self.carve(1024, shape=(2, 512)), self.res(f"gt{i}")) for i in range(2)]
        tmp = [(self.carve(512), self.res(f"tmp{i}")) for i in range(4)]
        self.load(attnT, self.attnT_s.rearrange("(h p) t -> p h t", p=128), a0r, reads=[self.res("attnT_s")])
        self.load(convT, self.convT_s.rearrange("(h p) t -> p h t", p=128), a0r,
                  reads=[self.res(("conv", ch)) for ch in range(16)])
        P.op("dve", lambda e: e.memset(ssqp, 0.0), [], [ssr])
        gi = 0
        for gidx, cg in enumerate(range(0, D, 256)):
            wa3, war = wbr[(gidx % 2) * 2]
            wc3, wcr = wbr[(gidx % 2) * 2 + 1]
            P.dma("pool", [(wa3, self.w_bra[:, cg:cg + 256].rearrange("(kc p) n -> p kc n", p=128))], writes=[war], key=war.name)
            P.dma("pool", [(wc3, self.w_brc[:, cg:cg + 256].rearrange("(kc p) n -> p kc n", p=128))], writes=[wcr], key=wcr.name)
            for m in range(2):
                c = cg // 128 + m
                for nh in range(2):
                    gt, gtr = gts[gi % 2]
                    gi += 1
                    tsl = slice(nh * 512, (nh + 1) * 512)
                    P.dma("sp", [(gt[:, 0, :], self.gA_s[c * 128:(c + 1) * 128, tsl]), (gt[:, 1, :], self.gC_s[c * 128:(c + 1) * 128, tsl])],
                          reads=[self.res(("g", 0, c)), self.res(("g", 1, c))], writes=[gtr], key=gtr.name)
                    ba, bar = self.bank()
                    for kc in range(16):
                        self.mm(ba[:], wa3[:, kc, m * 128:(m + 1) * 128], attnT[:, kc, tsl], kc == 0, kc == 15, [war, a0r], [bar])
                    bc_, bcr = self.bank()
                    for kc in range(16):
                        self.mm(bc_[:], wc3[:, kc, m * 128:(m + 1) * 128], convT[:, kc, tsl], kc == 0, kc == 15, [wcr, a0r], [bcr])
                    t0, t0r = tmp[(gi % 2) * 2]
                    t1, t1r = tmp[(gi % 2) * 2 + 1]
                    self.tt("dve", t0, ba[:], gt[:, 0, :], ALU.mult, [bar, gtr], [t0r])
                    self.tt("dve", t1, bc_[:], gt[:, 1, :], ALU.mult, [bcr, gtr], [t1r])
                    self.tt("dve", mergedT[:, c, tsl], t0, t1, ALU.add, [t0r, t1r], [mr])
        P.barrier()
        self.off = mark
        wts = [(wbuf[:, i * 4096:(i + 1) * 4096].bitcast(BF16).rearrange("p (k n) -> p k n", k=32), self.res(f"wo{i}")) for i in range(2)]
        xts = [(self.carve(1024), self.res(f"x4_{i}")) for i in range(2)]
        hst = [(self.carve(1024), self.res(f"h4_{i}")) for i in range(2)]
        h16 = [(self.carve(512), self.res(f"h16_{i}")) for i in range(2)]
        sq = self.carve(1024)
        sqr = self.res("sq")
        rstd = self.carve(1024)
        rr = self.res("rstd")
        for gidx, cg in enumerate(range(0, D, 256)):
            w3, wr = wts[gidx % 2]
            P.dma("pool", [(w3, self.w_out[:, cg:cg + 256].rearrange("(kc p) n -> p kc n", p=128))], writes=[wr], key=wr.name)
            for m in range(2):
                c = cg // 128 + m
                xt, xr = xts[c % 2]
                hs, hr = hst[c % 2]
                self.load(xt, self.xT[c * 128:(c + 1) * 128, :], xr)
                for nh in range(2):
                    bk, br = self.bank()
                    for kc in range(32):
                        self.mm(bk[:], w3[:, kc, m * 128:(m + 1) * 128], mergedT[:, kc, nh * 512:(nh + 1) * 512], kc == 0, kc == 31, [wr, mr], [br])
                    self.tt("dve", hs[:, nh * 512:(nh + 1) * 512], bk[:], xt[:, nh * 512:(nh + 1) * 512], ALU.add, [br, xr], [hr])
                self.A(sq, hs, AF.Square, [hr], [sqr])
                self.tt("dve", ssqp, ssqp, sq, ALU.add, [sqr, ssr], [ssr])
                self.store(self.h1T_s[c * 128:(c + 1) * 128, :], hs, self.res(("h1", c)), hr)
        for nh in range(2):
            bk, br = self.bank()
            self.mm(bk[:], self.ones32, ssqp[:, nh * 512:(nh + 1) * 512], True, True, [cst, ssr], [br])
            self.A(sq[:, nh * 512:(nh + 1) * 512], bk[:], AF.Sqrt, [br, cst], [sqr], scale=1.0 / D, bias=self.epsc)
        P.op("dve", lambda e: e.reciprocal(out=rstd, in_=sq), [sqr], [rr])
        def ld4(c):
            hs, hr = hst[c % 2]
            self.load(hs, self.h1T_s[c * 128:(c + 1) * 128, :], hr, reads=[self.res(("h1", c))])
        ld4(0)
        ld4(1)
        for c in range(32):
            hs, hr = hst[c % 2]
            hb, hbr = h16[c % 2]
            self.stt(hb.bitcast(BF16), hs, self.gffn[:, c:c + 1], rstd, ALU.mult, ALU.mult, [hr, rr, cst], [hbr])
            if c + 2 < 32:
                ld4(c + 2)
            self.store(self.hn2T_s[c * 128:(c + 1) * 128, :], hb.bitcast(BF16), self.res("hn2T_s"), hbr)

    def phase5(self):
        P = self.P
        cst = self.res("const")
        hn2 = self.carve(16384, BF16, shape=(32, 1024))
        hr_ = self.res("hn2")
        qpT = self.carve(8192, BF16, shape=(16, 1024))
        qpr = self.res("qpT")
        wts = [(self.carve(2048, BF16, shape=(32, 128)), self.res(f"wt{i}")) for i in range(2)]
        self.load(hn2, self.hn2T_s.rearrange("(c p) t -> p c t", p=128), hr_, reads=[self.res("hn2T_s")])
        wi = 0
        NCHAIN = 4
        cb_ = []
        for k in range(NCHAIN):
            cb_.append(dict(v12=self.carve(32, shape=(2, 16)), tmp128=self.carve(128), cand=self.carve(256, shape=(16, 16)),
                            cand2=self.carve(256), c24=self.carve(24), e16=self.carve(16), r=self.res(f"s5_{k}")))
        sc = self.carve(8 * 8 * 16)
        scrs = [self.res(f"sc5_{k}") for k in range(NCHAIN)]
        Es = [(self.carve(256, BF16), self.res(f"E{i}")) for i in range(4)]
        Gps = [(self.carve(2048, BF16, shape=(8, 512)), self.res(f"Gp{i}")) for i in range(2)]
        GTb = self.carve(4096, shape=(4, 1024))
        gtr = self.res("GTb")
        gls2 = [(self.carve(1024), self.res(f"gl{i}")) for i in range(2)]
        ast = [(self.carve(512), self.res(f"ast{i}")) for i in range(2)]
        keys = self.keys16

        def chain(tt, h, B, bkbr):
            v12, tmp128, cand, cand2, c24, e16, s5 = B["v12"], B["tmp128"], B["cand"], B["cand2"], B["c24"], B["e16"], B["r"]
            scr_ = scrs[tt % NCHAIN]
            so = (tt * 8 + h) * 16
            bk, br = bkbr
            for half in range(2):
                sv = bk[:, half * 128:(half + 1) * 128]
                P.op("dve", lambda e, half=half, sv=sv: e.max(out=v12[:, half, 0:8], in_=sv), [br], [s5], hz=True)
                yield
                P.op("dve", lambda e, half=half, sv=sv: e.match_replace(out=tmp128, in_to_replace=v12[:, half, 0:8], in_values=sv, imm_value=-1e30), [br, s5], [s5], hz=True)
                yield
                P.op("dve", lambda e, half=half, sv=sv: e.max(out=v12[:, half, 8:16], in_=tmp128), [s5], [s5], hz=True)
                yield
            self.tt("dve", cand, v12[:, 0, :].unsqueeze(2).broadcast_to([128, 16, 16]),
                    v12[:, 1, :].unsqueeze(1).broadcast_to([128, 16, 16]), ALU.add, [s5], [s5])
            yield
            candf = cand.rearrange("p a b -> p (a b)")
            P.op("dve", lambda e: e.max(out=c24[:, 0:8], in_=candf), [s5], [s5], hz=True)
            yield
            P.op("dve", lambda e: e.match_replace(out=cand2, in_to_replace=c24[:, 0:8], in_values=candf, imm_value=-1e30), [s5], [s5], hz=True)
            yield
            P.op("dve", lambda e: e.max(out=c24[:, 8:16], in_=cand2), [s5], [s5], hz=True)
            yield
            P.op("dve", lambda e: e.match_replace(out=candf, in_to_replace=c24[:, 8:16], in_values=cand2, imm_value=-1e30), [s5], [s5], hz=True)
            yield
            P.op("dve", lambda e: e.max(out=c24[:, 16:24], in_=candf), [s5], [s5], hz=True)
            yield
            s_tau = sc[:, so + 0:so + 1]
            s_negm = sc[:, so + 1:so + 2]
            s_Z = sc[:, so + 2:so + 3]
            s_lnz = sc[:, so + 3:so + 4]
            s_shift = sc[:, so + 4:so + 5]
            self.tt("dve", s_tau, c24[:, 15:16], c24[:, 16:17], ALU.add, [s5], [scr_])
            yield
            self.ts("dve", s_tau, s_tau, 0.5, None, ALU.mult, None, [scr_], [scr_])
            yield
            self.ts("dve", s_negm, c24[:, 0:1], -1.0, None, ALU.mult, None, [s5], [scr_])
            yield
            self.A(e16, c24[:, 0:16], AF.Exp, [s5, scr_], [s5, scr_], bias=s_negm, accum=s_Z)
            yield
            self.A(s_lnz, s_Z, AF.Ln, [scr_], [scr_])
            yield
            self.tt("dve", s_shift, s_negm, s_lnz, ALU.subtract, [scr_], [scr_])
            yield

        for h in range(8):
            for c in (2 * h, 2 * h + 1):
                w3, wr = wts[wi % 2]
                wi += 1
                P.dma("pool", [(w3, self.w_q[:, c * 128:(c + 1) * 128].rearrange("(kc p) n -> p kc n", p=128))], writes=[wr], key=wr.name)
                for nh in range(2):
                    bk, br = self.bank()
                    for kc in range(32):
                        self.mm(bk[:], w3[:, kc, :], hn2[:, kc, nh * 512:(nh + 1) * 512], kc == 0, kc == 31, [wr, hr_], [br])
                    self.copy(self.eveng(), qpT[:, c, nh * 512:(nh + 1) * 512], bk[:], [br], [qpr])
            for t0 in range(0, 8, NCHAIN):
                gens = []
                for k in range(NCHAIN):
                    tt = t0 + k
                    tsl = slice(tt * 128, (tt + 1) * 128)
                    bk, br = self.bank()
                    for half in range(2):
                        self.mm(bk[:, half * 128:(half + 1) * 128], qpT[:, 2 * h + half, tsl], keys[:, 2 * h + half, :], True, True, [qpr, cst], [br])
                    gens.append(chain(tt, h, cb_[k], (bk, br)))
                while gens:
                    for g in list(gens):
                        try:
                            next(g)
                        except StopIteration:
                            gens.remove(g)
        GTbs = [(GTb, gtr), (self.carve(4096, shape=(4, 1024)), self.res("GTb1"))]

        bctr5 = {"s": 0, "u": 0, "t": 0}

        def pbank5(kind):
            lo, cnt = {"s": (0, 5), "u": (5, 2), "t": (7, 1)}[kind]
            b = self.banks[lo + bctr5[kind] % cnt]
            bctr5[kind] += 1
            return b

        def front(n):
            ig, tt = n // 8, n % 8
            tsl = slice(tt * 128, (tt + 1) * 128)
            Gp, Gpr = Gps[n % 2]
            for h in range(8):
                so = (tt * 8 + h) * 16
                scr_ = scrs[tt % NCHAIN]
                E, Er = Es[h % 4]
                bk, br = self.bank()
                o3 = bk[:].rearrange("p (i j) -> p i j", i=4)
                self.mm(o3, qpT[:, 2 * h, tsl], keys[:, 2 * h, ig * 4:(ig + 1) * 4].unsqueeze(2).broadcast_to([128, 4, 128]),
                        True, False, [qpr, cst], [br])
                self.mm(o3, qpT[:, 2 * h + 1, tsl], keys[:, 2 * h + 1, :].unsqueeze(1).broadcast_to([128, 4, 128]),
                        False, True, [qpr, cst], [br])
                self.A(E, bk[:], AF.Exp, [br, scr_], [Er], bias=sc[:, so + 4:so + 5])
                self.stt(Gp[:, h, :], bk[:], sc[:, so:so + 1], E, ALU.is_ge, ALU.mult, [br, Er], [Gpr], sreads=[scr_])
            self.tt("dve", Gp[:, 0:4, :], Gp[:, 0:4, :], Gp[:, 4:8, :], ALU.add, [Gpr], [Gpr])
            self.tt("dve", Gp[:, 0:2, :], Gp[:, 0:2, :], Gp[:, 2:4, :], ALU.add, [Gpr], [Gpr])
            self.tt("dve", Gp[:, 0, :], Gp[:, 0, :], Gp[:, 1, :], ALU.add, [Gpr], [Gpr])

        def back(n):
            ig, tt = n // 8, n % 8
            tsl = slice(tt * 128, (tt + 1) * 128)
            Gp, Gpr = Gps[n % 2]
            gb_, gbr_ = GTbs[ig % 2]
            bk, br = self.bank()
            for ii in range(4):
                self.mm(bk[:, ii * 128:(ii + 1) * 128], Gp[:, 0, ii * 128:(ii + 1) * 128], self.ident16, True, True, [Gpr, cst], [br])
            self.copy("act", gb_[:, :, tsl], bk[:].rearrange("p (i t) -> p i t", i=4), [br], [gbr_])

        ustate = {}

        def ugrp(ig, k):
            nonlocal wi
            q, nh = k // 2, k % 2
            i = ig * 4 + q
            gb_, gbr_ = GTbs[ig % 2]
            if nh == 0:
                w3, wr = wts[wi % 2]
                wi += 1
                P.dma("pool", [(w3, self.uT[:, i * 128:(i + 1) * 128].rearrange("(kc p) n -> p kc n", p=128))], writes=[wr], key=wr.name)
                ustate["w"] = (w3, wr)
            w3, wr = ustate["w"]
            bk, br = self.bank()
            for kc in range(32):
                self.mm(bk[:], w3[:, kc, :], hn2[:, kc, nh * 512:(nh + 1) * 512], kc == 0, kc == 31, [wr, hr_], [br])
            gl, glr = gls2[q % 2]
            self.A(gl[:, nh * 512:(nh + 1) * 512], bk[:], AF.Gelu, [br], [glr])
            if nh == 1:
                a16, a16r = ast[i % 2]
                self.tt("dve", a16.bitcast(BF16), gl, gb_[:, q, :], ALU.mult, [glr, gbr_], [a16r])
                self.store(self.actT_s[i * 128:(i + 1) * 128, :], a16.bitcast(BF16), self.res(("act", i)), a16r)

        NS = 32 * 8
        for n in range(NS):
            front(n)
            if n >= 8:
                ugrp(n // 8 - 1, n % 8)
            if n >= 1:
                back(n - 1)
        back(NS - 1)
        for k in range(8):
            ugrp(31, k)

    def phase6(self):
        P = self.P
        cst = self.res("const")
        acts = [(self.carve(4096, BF16, shape=(8, 1024)), self.res(f"ac{i}")) for i in range(3)]
        vts = [(self.carve(2048, BF16, shape=(8, 512)), self.res(f"vw{i}")) for i in range(3)]
        hst = [(self.carve(1024), self.res(f"h6_{i}")) for i in range(2)]
        ost = [(self.carve(1024), self.res(f"o6_{i}")) for i in range(2)]
        sq = self.carve(1024)
        sqr = self.res("sq6")
        ssqp = self.carve(1024)
        ssr = self.res("ssq6")
        rstd = self.carve(1024)
        rr = self.res("rstd6")
        P.op("dve", lambda e: e.memset(ssqp, 0.0), [], [ssr])
        NRES = 5
        resid = [(self.carve(4096, BF16, shape=(8, 1024)), self.res(f"acres{i}")) for i in range(NRES)]
        li = 0
        for ps in range(8):
            col0 = ps * 512
            bks = [self.bank() for _ in range(8)]
            for eg in range(16):
                vw, vwr = vts[li % 3]
                e0 = eg * 8
                if eg < NRES:
                    ac, acr = resid[eg]
                else:
                    ac, acr = acts[li % 3]
                li += 1
                if eg >= NRES or ps == 0:
                    P.dma("sp", [(ac, self.actT_s[e0 * 128:(e0 + 8) * 128, :].rearrange("(e p) t -> p e t", p=128))],
                          reads=[self.res(("act", e0 + k)) for k in range(8)] if ps == 0 else (), writes=[acr], key=acr.name)
                P.dma("pool", [(vw, self.pv[e0 * 128:(e0 + 8) * 128, col0:col0 + 512].rearrange("(e p) n -> p e n", p=128))],
                      writes=[vwr], key=vwr.name)
                for k in range(8):
                    e = e0 + k
                    for cc in range(4):
                        for nh in range(2):
                            bk, br = bks[cc * 2 + nh]
                            self.mm(bk[:], vw[:, k, cc * 128:(cc + 1) * 128], ac[:, k, nh * 512:(nh + 1) * 512], e == 0, e == 127, [vwr, acr], [br])
            for cc in range(4):
                c = ps * 4 + cc
                hs, hr = hst[c % 2]
                os_, osr = ost[c % 2]
                self.load(hs, self.h1T_s[c * 128:(c + 1) * 128, :], hr)
                for nh in range(2):
                    bk, br = bks[cc * 2 + nh]
                    self.tt("dve", os_[:, nh * 512:(nh + 1) * 512], bk[:], hs[:, nh * 512:(nh + 1) * 512], ALU.add, [br, hr], [osr])
                self.A(sq, os_, AF.Square, [osr], [sqr])
                self.tt("dve", ssqp, ssqp, sq, ALU.add, [sqr, ssr], [ssr])
                self.store(self.h2T_s[c * 128:(c + 1) * 128, :], os_, self.res(("h2", c)), osr)
        for nh in range(2):
            bk, br = self.bank()
            self.mm(bk[:], self.ones32, ssqp[:, nh * 512:(nh + 1) * 512], True, True, [cst, ssr], [br])
            self.A(sq[:, nh * 512:(nh + 1) * 512], bk[:], AF.Sqrt, [br, cst], [sqr], scale=1.0 / D, bias=self.epsc)
        P.op("dve", lambda e: e.reciprocal(out=rstd, in_=sq), [sqr], [rr])
        def ld6(c):
            hs, hr = hst[c % 2]
            self.load(hs, self.h2T_s[c * 128:(c + 1) * 128, :], hr, reads=[self.res(("h2", c))])
        ld6(0)
        ld6(1)
        for c in range(32):
            hs, hr = hst[c % 2]
            os_, osr = ost[c % 2]
            self.stt(os_, hs, self.gfin[:, c:c + 1], rstd, ALU.mult, ALU.mult, [hr, rr, cst], [osr])
            if c + 2 < 32:
                ld6(c + 2)
            self.store(self.outT[c * 128:(c + 1) * 128, :], os_, self.res(("out", c)), osr)


def _t5_bucket(dist):
    dist = np.asarray(dist, dtype=np.int64)
    max_exact = 16
    d32 = np.maximum(dist, 1).astype(np.float32)
    val = np.log(d32 / np.float32(max_exact)) / np.float32(math.log(128 / max_exact)) * np.float32(32 - max_exact)
    large = max_exact + val.astype(np.int32)
    large = np.minimum(large, 31)
    return np.where(dist < max_exact, dist, large)


def _pcol(vec, C):
    return np.ascontiguousarray(np.asarray(vec, np.float32).reshape(C, 128).T)


def prepare_inputs(x, norm_mix, w_in, conv_w, conv_b, w_br_attn, w_br_conv, b_gate, rel_bias,
                   w_out, norm_ffn, peer_w_q, peer_sub_keys, peer_u, peer_v, norm_final):
    f = lambda a: np.ascontiguousarray(np.asarray(a, dtype=np.float32))
    x = f(x)
    rel_bias = f(rel_bias)
    shared = {
        "w_in": f(w_in[0]), "w_bra": f(w_br_attn[0]), "w_brc": f(w_br_conv[0]), "w_out": f(w_out[0]),
        "w_q": f(peer_w_q[0]), "uT": np.ascontiguousarray(f(peer_u[0]).T), "pv": f(peer_v[0]),
        "keysT": np.ascontiguousarray(np.transpose(f(peer_sub_keys[0]), (3, 0, 1, 2)).reshape(128, 16 * 128)),
        "gmixb": f(norm_mix[0]).reshape(1, D),
        "identd": np.eye(128, dtype=np.float32),
    }
    qi = np.arange(128)[:, None]
    kj = np.arange(512)[None, :]
    nearb = np.zeros((2, 16, 128, 512), np.float32)
    cmask = np.zeros((128, 2, 512), np.float32)
    for par in range(2):
        dist = (256 + par * 128 + qi) - kj
        bk = _t5_bucket(np.maximum(dist, 0))
        nearb[par] = np.transpose(rel_bias[bk], (2, 0, 1))
        cmask[:, par, :] = np.where(dist >= 0, 0.0, BIGNEG)
    shared["nearb"] = np.ascontiguousarray(nearb.reshape(2 * 16 * 128, 512))
    cw = f(conv_w[0])[:, 0, :]
    convw = np.ascontiguousarray(np.transpose(cw.reshape(3, 16, 128), (2, 1, 0))).reshape(128, 48)
    bg = np.ascontiguousarray(np.transpose(f(b_gate[0]).reshape(2, 32, 128), (2, 0, 1))).reshape(128, 64)
    in_maps = []
    for c in range(8):
        b, half = c // 2, c % 2
        own = x[b, half * 1024:(half + 1) * 1024]
        prev = x[b, 0:1024] if half == 1 else np.zeros_like(own)
        pastb = np.zeros((128, 8, 8), np.float32)
        valid = np.zeros((128, 8, 8), np.float32)
        ownm = np.zeros((128, 8, 8), np.float32)
        for i in range(8):
            cb = 4 + i // 2
            for n in range(8):
                ok = (n < cb) and (half == 1 or n >= 4)
                pastb[:, i, n] = 0.0 if ok else -1e30
                valid[:, i, n] = 1.0 if ok else 0.0
                ownm[:, i, n] = 1.0 if n == cb else 0.0
        small = np.concatenate([
            _pcol(norm_mix[0], 32), _pcol(norm_ffn[0], 32), _pcol(norm_final, 32),
            convw, _pcol(conv_b[0], 16), bg,
            np.tile(rel_bias[31][None, :], (128, 1)),
            pastb.reshape(128, 64), valid.reshape(128, 64), ownm.reshape(128, 64),
            cmask.reshape(128, 1024), np.ones((128, 128), np.float32), np.full((128, 1), EPS, np.float32),
        ], axis=1)
        pad = np.zeros((128, 1600 - small.shape[1]), np.float32)
        m = dict(shared)
        m["xctx"] = np.ascontiguousarray(np.concatenate([prev, own], axis=0))
        m["xT"] = np.ascontiguousarray(own.T)
        m["smallc"] = np.ascontiguousarray(np.concatenate([small, pad], axis=1))
        in_maps.append(m)
    return in_maps


def kernel(**inputs):
    in_maps = prepare_inputs(**inputs)
    nc = Builder().build()
    res = run_bass_kernel_spmd(nc, in_maps, core_ids=list(range(8)))
    out = np.empty((4, 2048, D), np.float32)
    for c in range(8):
        b, half = c // 2, c % 2
        out[b, half * 1024:(half + 1) * 1024, :] = res.results[c]["outT"].T
    return out
```

```python
import math
from contextlib import ExitStack
import numpy as np
import concourse.bass as bass
import concourse.mybir as mybir
from concourse.bass_utils import run_bass_kernel_spmd

F32 = mybir.dt.float32
BF16 = mybir.dt.bfloat16
AF = mybir.ActivationFunctionType
ALU = mybir.AluOpType
AX = mybir.AxisListType

SEM_LIMIT = 24000
ENGS = ("pe", "act", "dve", "pool", "sp")


class Res:
    __slots__ = ("name", "w", "r", "hz")

    def __init__(self, name):
        self.name = name
        self.w = None
        self.r = {}
        self.hz = False


class Prog:
    def __init__(self, nc, stack, n_sems=100):
        self.nc = nc
        self.sems = [stack.enter_context(nc.semaphore(f"s{i}")) for i in range(n_sems)]
        self.next_sem = 0
        self.q = {e: [] for e in ENGS}
        self.esem = {}
        self.ecnt = {}
        self.allsem = {}
        for e in ENGS:
            self._new_esem(e)
        self.waited = {e: {} for e in ENGS}
        self.ksem = {}
        self.free_ks = []

    def _alloc_sem(self):
        i = self.next_sem
        self.next_sem += 1
        assert i < len(self.sems), "out of semaphores"
        return i

    def _new_esem(self, e):
        self.esem[e] = self._alloc_sem()
        self.ecnt[e] = 0

    def _deps(self, eng, reads, writes, skip_own=False, sreads=()):
        need = {}
        own = self.esem[eng]
        strict_own = 0

        def add(s, v):
            if need.get(s, 0) < v:
                need[s] = v
        for r in reads:
            if r.w is not None:
                add(*r.w)
                if r.w[0] == own:
                    strict_own = max(strict_own, r.w[1])
        for r in sreads:
            if r.w is not None:
                add(*r.w)
                if r.w[0] == own:
                    strict_own = max(strict_own, r.w[1])
        for w in writes:
            if w.w is not None:
                add(*w.w)
            for s, v in w.r.items():
                add(s, v)
        if own in need:
            if skip_own or strict_own == 0:
                del need[own]
            else:
                need[own] = strict_own
        return self._filter(eng, need, False)

    def _filter(self, eng, need, skip_own=False):
        out = []
        wd = self.waited[eng]
        for s, v in need.items():
            if skip_own and s == self.esem[eng]:
                continue
            if wd.get(s, 0) >= v:
                continue
            wd[s] = v
            out.append((s, v))
        return out

    def _commit(self, ev, reads, writes):
        s, v = ev
        self.allsem[s] = max(self.allsem.get(s, 0), v)
        for r in reads:
            if r.r.get(s, 0) < v:
                r.r[s] = v
        for w in writes:
            w.w = ev
            w.r = {}

    def op(self, eng, fn, reads=(), writes=(), sreads=(), hz=False):
        if self.ecnt[eng] >= SEM_LIMIT:
            self._new_esem(eng)
        waits = self._deps(eng, reads, writes, skip_own=(eng == "pe"), sreads=sreads)
        self.ecnt[eng] += 1
        ev = (self.esem[eng], self.ecnt[eng])
        self.q[eng].append((waits, fn, ev[0], 1))
        self._commit(ev, list(reads) + list(sreads), writes)
        if hz:
            for w in writes:
                w.hz = True
        return ev

    def dma(self, queue, pieces, reads=(), writes=(), key=None):
        ks = self.ksem.get(key)
        if ks is None or ks[1] + 16 * len(pieces) >= SEM_LIMIT:
            while self.free_ks and self.free_ks[-1][1] + 16 * len(pieces) >= SEM_LIMIT:
                self.free_ks.pop()
            ks = self.free_ks.pop() if self.free_ks else [self._alloc_sem(), 0]
            self.ksem[key] = ks
        waits = self._deps(queue, reads, writes)
        for i, (o, a) in enumerate(pieces):
            fn = (lambda e, o=o, a=a: e.dma_start(out=o, in_=a))
            self.q[queue].append((waits if i == 0 else [], fn, ks[0], 16))
        ks[1] += 16 * len(pieces)
        ev = (ks[0], ks[1])
        self._commit(ev, reads, writes)
        return ev

    def barrier(self):
        for e in ENGS:
            need = dict(self.allsem)
            waits = self._filter(e, need, skip_own=True)
            if waits:
                self.q[e].append((waits, None, None, 0))
        self.free_ks.extend(self.ksem.values())
        self.ksem = {}

    def emit(self):
        nc = self.nc
        sems = self.sems
        engmap = {"pe": "tensor", "act": "scalar", "dve": "vector", "pool": "gpsimd", "sp": "sync"}
        with nc.Block() as block:
            for e in ENGS:
                ops = self.q[e]
                if not ops:
                    continue

                def body(eng, ops=ops):
                    for waits, fn, s, inc in ops:
                        for (ws, wv) in waits:
                            eng.wait_ge(sems[ws], wv)
                        if fn is not None:
                            fn(eng).then_inc(sems[s], inc)
                getattr(block, engmap[e])(body)


D = 4096
T = 1024
NCH = 32
SCALE = 128 ** -0.5
EPS = 1e-6
BIGNEG = -30000.0
W_IN = 20480


class Builder:
    def __init__(self, stage=99, debug=False):
        self.stage = stage
        self.debug = debug
        self.nc = bass.Bass("TRN2", target_bir_lowering=False)
        self.st = ExitStack()
        self.R = {}
        self.bi = 0
        self.evi = 0

    def res(self, key):
        r = self.R.get(key)
        if r is None:
            r = Res(str(key))
            self.R[key] = r
        return r

    def din(self, name, shape):
        return self.nc.dram_tensor(name, list(shape), F32, kind="ExternalInput").ap()

    def dscr(self, name, shape, dt):
        kind = "ExternalOutput" if (self.debug and name in self.debug) else "Internal"
        return self.nc.dram_tensor(name, list(shape), dt, kind=kind).ap()

    def carve(self, nwords, dt=F32, shape=None):
        v = self.arena[:, self.off:self.off + nwords]
        self.off += nwords
        assert self.off <= self.arena_words, f"arena overflow {self.off}"
        if dt != F32:
            v = v.bitcast(dt)
        if shape is not None:
            if len(shape) == 2:
                v = v.rearrange("p (a b) -> p a b", a=shape[0])
            elif len(shape) == 3:
                v = v.rearrange("p (a b c) -> p a b c", a=shape[0], b=shape[1])
        return v

    def bank(self):
        b = self.banks[self.bi % 8]
        self.bi += 1
        return b

    def eveng(self):
        self.evi += 1
        return "act" if self.evi % 2 == 0 else "dve"

    def mm(self, out, lhsT, rhs, start, stop, reads, writes):
        self.P.op("pe", lambda e: e.matmul(out, lhsT, rhs, start=start, stop=stop), reads, writes)

    def tr(self, out, in_, reads, writes):
        idn = self.ident16
        self.P.op("pe", lambda e: e.transpose(out, in_, idn), list(reads) + [self.res("const")], writes)

    def A(self, out, in_, func, reads, writes, bias=None, scale=None, accum=None):
        kw = {}
        if bias is not None:
            kw["bias"] = bias
        if scale is not None:
            kw["scale"] = scale
        if accum is not None:
            kw["accum_out"] = accum
        self.P.op("act", lambda e: e.activation(out=out, in_=in_, func=func, **kw), reads, writes, hz=(accum is not None))

    def copy(self, eng, out, in_, reads, writes):
        if eng == "act":
            self.A(out, in_, AF.Copy, reads, writes)
        else:
            self.P.op(eng, lambda e: e.tensor_copy(out=out, in_=in_), reads, writes)

    def tt(self, eng, out, in0, in1, op, reads, writes):
        self.P.op(eng, lambda e: e.tensor_tensor(out=out, in0=in0, in1=in1, op=op), reads, writes)

    def ts(self, eng, out, in0, s1, s2, op0, op1, reads, writes, sreads=()):
        if op1 is None:
            self.P.op(eng, lambda e: e.tensor_scalar(out=out, in0=in0, scalar1=s1, scalar2=None, op0=op0), reads, writes, sreads=sreads)
        else:
            self.P.op(eng, lambda e: e.tensor_scalar(out=out, in0=in0, scalar1=s1, scalar2=s2, op0=op0, op1=op1), reads, writes, sreads=sreads)

    def stt(self, out, in0, scalar, in1, op0, op1, reads, writes, sreads=()):
        self.P.op("dve", lambda e: e.scalar_tensor_tensor(out=out, in0=in0, scalar=scalar, in1=in1, op0=op0, op1=op1),
                  reads, writes, sreads=sreads)

    def red(self, out, in_, op, reads, writes):
        self.P.op("dve", lambda e: e.tensor_reduce(out=out, in_=in_, axis=AX.X, op=op), reads, writes)

    def load(self, out, in_, res, reads=(), queue="sp"):
        self.P.dma(queue, [(out, in_)], reads=reads, writes=[res], key=res.name)

    def store(self, out, in_, dres, sres):
        self.P.dma("sp", [(out, in_)], reads=[sres], writes=[dres], key="st_" + sres.name)

    def build(self):
        nc = self.nc
        st = self.st
        with st:
            self.P = Prog(nc, st)
            self.arena_words = 51200
            self.arena = st.enter_context(nc.sbuf_tensor("arena", [128, self.arena_words], F32))
            self.banks = []
            for i in range(8):
                t = st.enter_context(nc.psum_tensor(f"bank{i}", [128, 512], F32))
                self.banks.append((t, Res(f"bank{i}")))
            self.off = 0
            self._declare_io()
            self._consts()
            self.base = self.off
            stages = [self.phase12, self.phase3, self.phase4, self.phase5, self.phase6]
            for i, f in enumerate(stages):
                if self.stage > i:
                    self.off = self.base
                    f()
                    self.P.barrier()
            self.P.barrier()
            self.P.emit()
        return nc

    def _declare_io(self):
        d = self.din
        self.xctx = d("xctx", [2048, D])
        self.xT = d("xT", [D, T])
        self.w_in = d("w_in", [D, W_IN])
        self.w_bra = d("w_bra", [2048, D])
        self.w_brc = d("w_brc", [2048, D])
        self.w_out = d("w_out", [D, D])
        self.w_q = d("w_q", [D, 2048])
        self.uT = d("uT", [D, 16384])
        self.pv = d("pv", [16384, D])
        self.keysT = d("keysT", [128, 16 * 128])
        self.nearb = d("nearb", [2 * 16 * 128, 512])
        self.smallc = d("smallc", [128, 1600])
        self.gmixb = d("gmixb", [1, D])
        self.identd = d("identd", [128, 128])
        s = self.dscr
        self.qT_s = s("qT_s", [2048, T], BF16)
        self.kT_s = s("kT_s", [2048, 2048], BF16)
        self.v_s = s("v_s", [2048, 2048], BF16)
        self.convT_s = s("convT_s", [2048, T], BF16)
        self.gA_s = s("gA_s", [D, T], F32)
        self.gC_s = s("gC_s", [D, T], F32)
        self.attnT_s = s("attnT_s", [2048, T], BF16)
        self.h1T_s = s("h1T_s", [D, T], F32)
        self.hn2T_s = s("hn2T_s", [D, T], BF16)
        self.GT_s = s("GT_s", [16384, T], F32)
        self.actT_s = s("actT_s", [16384, T], BF16)
        self.h2T_s = s("h2T_s", [D, T], F32)
        self.outT = nc_out = self.nc.dram_tensor("outT", [D, T], F32, kind="ExternalOutput").ap()

    def _consts(self):
        P = self.P
        c = self.res("const")
        self.ident16 = self.carve(64, BF16)
        self.keys16 = self.carve(1024, BF16, shape=(16, 128))
        self.small = self.carve(1600)
        sm = self.small
        o = 0

        def take(n, shape=None):
            nonlocal o
            v = sm[:, o:o + n]
            o += n
            if shape is not None:
                if len(shape) == 2:
                    v = v.rearrange("p (a b) -> p a b", a=shape[0])
            return v
        self.gmix = take(32)
        self.gffn = take(32)
        self.gfin = take(32)
        self.convw = take(48, (16, 3))
        self.convb = take(16)
        self.bgate = take(64, (2, 32))
        self.cfar = take(16)
        self.pastb = take(64, (8, 8))
        self.valid01 = take(64, (8, 8))
        self.own01 = take(64, (8, 8))
        self.cmask = take(1024, (2, 512))
        self.ones32 = take(128)
        self.epsc = take(1)
        P.dma("pool", [(self.ident16, self.identd)], writes=[c], key="c_id")
        P.dma("pool", [(self.keys16, self.keysT.rearrange("p (a b) -> p a b", a=16))], writes=[c], key="c_keys")
        P.dma("sp", [(self.small, self.smallc)], writes=[c], key="c_small")
        self.misc = self.carve(512)
        P.barrier()

    def phase12(self):
        P = self.P
        hnT = self.carve(16384, BF16, shape=(32, 1024))
        hnr = self.res("hnT")
        hn_tail = self.carve(32, BF16, shape=(32, 2))
        tailr = self.res("hn_tail")
        mark = self.off
        xts = [(self.carve(4096), self.res(f"xt{i}")) for i in range(2)]
        gb = self.carve(4096)
        gbr = self.res("gb")
        xs = self.carve(2048, BF16)
        xsr = self.res("xs")
        stat = self.misc
        ssq, std, rstd = stat[:, 0:1], stat[:, 1:2], stat[:, 2:3]
        sr = self.res("stat")
        self.load(gb, self.gmixb.broadcast_to([128, D]), gbr)
        norm_end = self.off
        self.off = mark
        wts = [(self.carve(4096, BF16), self.res(f"wt{i}")) for i in range(2)]
        stg = [(self.carve(1024), self.res(f"stg{i}")) for i in range(3)]
        ccs = self.carve(1032)
        gated = self.carve(1032)
        yb = self.carve(1024)
        ccr, gr, yr = self.res("ccs"), self.res("gated"), self.res("yb")
        self.off = max(self.off, norm_end)
        self.wts = wts
        self.wi = 0
        self.si = 0
        cst = self.res("const")

        def norm_seg(seg):
            for tt in range(8):
                xt, xr = xts[tt % 2]
                self.load(xt, self.xctx[seg * 1024 + tt * 128: seg * 1024 + (tt + 1) * 128, :], xr)
                self.A(xs, xt, AF.Square, [xr], [xsr, sr], accum=ssq)
                self.A(std, ssq, AF.Sqrt, [sr, cst], [sr], scale=1.0 / D, bias=self.epsc)
                P.op("dve", lambda e: e.reciprocal(out=rstd, in_=std), [sr], [sr])
                self.stt(xs, xt, rstd, gb, ALU.mult, ALU.mult, [xr, gbr], [xsr], sreads=[sr])
                for cg in range(4):
                    bk, br = self.bank()
                    bk16 = bk[:].bitcast(BF16)
                    for j in range(8):
                        cc = cg * 8 + j
                        self.tr(bk16[:, j * 128:(j + 1) * 128], xs[:, cc * 128:(cc + 1) * 128], [xsr], [br])
                    self.copy(self.eveng(), hnT[:, cg * 8:(cg + 1) * 8, tt * 128:(tt + 1) * 128],
                              bk16.rearrange("p (c t) -> p c t", c=8), [br], [hnr])

        def wtile(KC, ncols):
            wt, wr = wts[self.wi % 2]
            self.wi += 1
            return wt[:, 0:KC * ncols].rearrange("p (k n) -> p k n", k=KC), wr
        self.wtile = wtile

        def stage_slot():
            s = stg[self.si % 3]
            self.si += 1
            return s

        def proj_fm(W, c0, ncols, epi, wcols=256, act=hnT, actr=hnr, KC=32, NT=1024):
            for g0 in range(c0, c0 + ncols, wcols):
                wt, wr = wtile(KC, wcols)
                P.dma("pool", [(wt, W[:, g0:g0 + wcols].rearrange("(kc p) n -> p kc n", p=128))], writes=[wr], key=wr.name)
                for m in range(wcols // 128):
                    for nh in range(NT // 512):
                        bk, br = self.bank()
                        for kc in range(KC):
                            self.mm(bk[:], wt[:, kc, m * 128:(m + 1) * 128], act[:, kc, nh * 512:(nh + 1) * 512],
                                    kc == 0, kc == KC - 1, [wr, actr], [br])
                        epi(g0 + m * 128, nh, bk, br)
        self.proj_fm = proj_fm
        self.stage_slot = stage_slot

        km32 = self.misc[:, 16:144].rearrange("p (h n) -> p h n", h=16)
        kmr = self.res("km")

        def epi_store_bf16(dst, col_base, tok0, kmeans=False):
            state = {}

            def epi(c0, nh, bk, br):
                if nh == 0:
                    state["slot"] = stage_slot()
                sl, slr = state["slot"]
                s16 = sl.bitcast(BF16)
                self.copy(self.eveng(), s16[:, nh * 512:(nh + 1) * 512], bk[:], [br], [slr])
                if nh == 1:
                    r0 = c0 - col_base
                    if kmeans:
                        hh = r0 // 128
                        b0 = tok0 // 256
                        self.red(km32[:, hh, b0:b0 + 4], s16[:, 0:1024].rearrange("p (n l) -> p n l", n=4), ALU.add, [slr], [kmr])
                    self.store(dst[r0:r0 + 128, tok0:tok0 + 1024], s16[:, 0:1024], self.res(("d", id(dst), r0)), slr)
            return epi

        def proj_v(seg):
            for g0 in range(4096, 6144, 256):
                wt, wr = wtile(32, 256)
                P.dma("pool", [(wt, self.w_in[:, g0:g0 + 256].rearrange("(kc p) n -> p kc n", p=128))], writes=[wr], key=wr.name)
                sl, slr = stage_slot()
                s16 = sl.bitcast(BF16).rearrange("p (t n) -> p t n", t=8)
                for tt in range(8):
                    bk, br = self.bank()
                    for kc in range(32):
                        self.mm(bk[:, 0:256], hnT[:, kc, tt * 128:(tt + 1) * 128], wt[:, kc, :], kc == 0, kc == 31, [wr, hnr], [br])
                    self.copy(self.eveng(), s16[:, tt, :], bk[:, 0:256], [br], [slr])
                self.store(self.v_s[seg * 1024:(seg + 1) * 1024, g0 - 4096:g0 - 4096 + 256].rearrange("(t p) n -> p t n", p=128),
                           s16, self.res(("v", seg, g0)), slr)

        norm_seg(0)
        self.copy("dve", hn_tail, hnT[:, :, 1022:1024], [hnr], [tailr])
        P.barrier()
        proj_fm(self.w_in, 2048, 2048, epi_store_bf16(self.kT_s, 2048, 0, kmeans=True))
        proj_v(0)
        P.barrier()
        self.load(gb, self.gmixb.broadcast_to([128, D]), gbr)
        norm_seg(1)
        P.barrier()
        proj_fm(self.w_in, 2048, 2048, epi_store_bf16(self.kT_s, 2048, 1024, kmeans=True))
        proj_v(1)
        proj_fm(self.w_in, 0, 2048, epi_store_bf16(self.qT_s, 0, 0))
        for ch in range(16):
            pcs = {}
            for nm, cbase in (("cc", 8192), ("cu", 10240), ("cb", 6144)):
                wt, wr = wtile(32, 128)
                g0 = cbase + ch * 128
                P.dma("pool", [(wt, self.w_in[:, g0:g0 + 128].rearrange("(kc p) n -> p kc n", p=128))], writes=[wr], key=wr.name)
                for nh in range(2):
                    bk, br = self.bank()
                    for kc in range(32):
                        self.mm(bk[:], wt[:, kc, :], hnT[:, kc, nh * 512:(nh + 1) * 512], kc == 0, kc == 31, [wr, hnr], [br])
                    pcs[(nm, nh)] = (bk, br)
                if nm != "cb":
                    bk, br = self.bank()
                    for kc in range(32):
                        self.mm(bk[:, 0:2], wt[:, kc, :], hn_tail[:, kc, :], kc == 0, kc == 31, [wr, tailr], [br])
                    pcs[(nm, 2)] = (bk, br)
                if nm == "cc":
                    for nh in range(2):
                        bk, br = pcs[("cc", nh)]
                        self.copy("act", ccs[:, 2 + nh * 512: 2 + (nh + 1) * 512], bk[:], [br], [ccr])
                    bk, br = pcs[("cc", 2)]
                    self.copy("act", ccs[:, 0:2], bk[:, 0:2], [br], [ccr])
                if nm == "cu":
                    for nh in range(2):
                        bk, br = pcs[("cu", nh)]
                        self.tt("dve", gated[:, 2 + nh * 512: 2 + (nh + 1) * 512], ccs[:, 2 + nh * 512: 2 + (nh + 1) * 512], bk[:],
                                ALU.mult, [br, ccr], [gr])
                    bk, br = pcs[("cu", 2)]
                    self.tt("dve", gated[:, 0:2], ccs[:, 0:2], bk[:, 0:2], ALU.mult, [br, ccr], [gr])
                    self.ts("dve", yb, gated[:, 2:1026], self.convw[:, ch, 2:3], self.convb[:, ch:ch + 1], ALU.mult, ALU.add,
                            [gr, cst], [yr])
                    self.stt(yb, gated[:, 1:1025], self.convw[:, ch, 1:2], yb, ALU.mult, ALU.add, [gr, cst, yr], [yr])
                    self.stt(yb, gated[:, 0:1024], self.convw[:, ch, 0:1], yb, ALU.mult, ALU.add, [gr, cst, yr], [yr])
                if nm == "cb":
                    sl, slr = stage_slot()
                    s16 = sl.bitcast(BF16)
                    for nh in range(2):
                        bk, br = pcs[("cb", nh)]
                        self.tt("dve", s16[:, nh * 512:(nh + 1) * 512], yb[:, nh * 512:(nh + 1) * 512], bk[:], ALU.mult,
                                [br, yr], [slr])
                    self.store(self.convT_s[ch * 128:(ch + 1) * 128, :], s16[:, 0:1024], self.res(("conv", ch)), slr)
        for which, dst in ((0, self.gA_s), (1, self.gC_s)):
            state = {}

            def epi(c0, nh, bk, br, which=which, dst=dst, state=state):
                if nh == 0:
                    state["slot"] = stage_slot()
                sl, slr = state["slot"]
                cidx = (c0 - 12288 - which * 4096) // 128
                self.A(sl[:, nh * 512:(nh + 1) * 512], bk[:], AF.Sigmoid, [br, cst], [slr], bias=self.bgate[:, which, cidx:cidx + 1])
                if nh == 1:
                    self.store(dst[cidx * 128:(cidx + 1) * 128, :], sl, self.res(("g", which, cidx)), slr)
            proj_fm(self.w_in, 12288 + which * 4096, 4096, epi)

    def phase3(self):
        P = self.P
        cst = self.res("const")
        qT = self.carve(8192, BF16, shape=(16, 1024))
        qr = self.res("qT")
        attnT = self.carve(8192, BF16, shape=(16, 1024))
        ar = self.res("attnT")
        negb = self.carve(1024, shape=(8, 16, 8))
        nbr = self.res("negb")
        km32 = self.misc[:, 16:144].rearrange("p (h n) -> p h n", h=16)
        km16 = self.carve(64, BF16, shape=(16, 8))
        kmr = self.res("km")
        kts = [(self.carve(1024, BF16), self.res(f"kt{i}")) for i in range(2)]
        vts = [(self.carve(1024, BF16, shape=(16, 128)), self.res(f"vt{i}")) for i in range(2)]
        nbs = [(self.carve(1024, shape=(2, 512)), self.res(f"nb{i}")) for i in range(2)]
        Ls = [(self.carve(2048), self.res(f"L{i}")) for i in range(3)]
        Ps = [(self.carve(1024, BF16), self.res(f"P{i}")) for i in range(3)]
        PTs = [(self.carve(1024, BF16, shape=(16, 128)), self.res(f"PT{i}")) for i in range(3)]
        gm = [self.carve(128, shape=(16, 8)) for _ in range(4)]
        gmr = self.res("gm")
        mx = self.carve(16)
        rs_all = self.carve(24)
        rsrs = [self.res(f"rs{i}") for i in range(3)]
        self.load(qT, self.qT_s.rearrange("(h p) t -> p h t", p=128), qr, reads=[self.res(("d", id(self.qT_s), r0)) for r0 in range(0, 2048, 128)])
        self.ts("dve", km16, km32, 1.0 / 256.0, None, ALU.mult, None, [kmr], [kmr])
        for i in range(8):
            bk, br = self.bank()
            for h in range(16):
                self.mm(bk[:, h * 8:(h + 1) * 8], qT[:, h, i * 128:(i + 1) * 128], km16[:, h, :], True, True, [qr, kmr], [br])
            pg = bk[:, 0:128].rearrange("p (h n) -> p h n", h=16)

            def bc(v):
                return v.unsqueeze(1).broadcast_to([128, 16, 8])

            def bm(v):
                return v.unsqueeze(2).broadcast_to([128, 16, 8])
            g0, g1, g2, eq = gm
            self.tt("dve", g0, pg, bc(self.pastb[:, i, :]), ALU.add, [br, cst], [gmr])
            self.red(mx, g0, ALU.max, [gmr], [gmr])
            self.tt("dve", eq, g0, bm(mx), ALU.is_equal, [gmr], [gmr])
            self.stt(g1, eq, -3e30, g0, ALU.mult, ALU.add, [gmr], [gmr])
            self.red(mx, g1, ALU.max, [gmr], [gmr])
            self.tt("dve", eq, g1, bm(mx), ALU.is_equal, [gmr], [gmr])
            self.stt(g2, eq, -3e30, g1, ALU.mult, ALU.add, [gmr], [gmr])
            self.red(mx, g2, ALU.max, [gmr], [gmr])
            self.tt("dve", eq, g0, bm(mx), ALU.is_ge, [gmr], [gmr])
            self.tt("dve", eq, eq, bc(self.valid01[:, i, :]), ALU.mult, [gmr, cst], [gmr])
            self.tt("dve", eq, eq, bc(self.own01[:, i, :]), ALU.add, [gmr, cst], [gmr])
            self.ts("dve", negb[:, i, :, :], eq, -1.0, -BIGNEG, ALU.add, ALU.mult, [gmr], [nbr])
        items = [(h, i) for h in range(16) for i in range(8)]
        bctr = {"s": 0, "t": 0, "v": 0}

        def pbank(kind):
            lo, n = {"s": (0, 4), "t": (4, 2), "v": (6, 2)}[kind]
            b = self.banks[lo + bctr[kind] % n]
            bctr[kind] += 1
            return b

        def stageA(idx):
            h, i = items[idx]
            kt, kr = kts[h % 2]
            nb, nr = nbs[h % 2]
            if i == 0:
                vt, vr = vts[h % 2]
                self.load(kt, self.kT_s[h * 128:(h + 1) * 128, :], kr)
                self.load(vt, self.v_s[:, h * 128:(h + 1) * 128].rearrange("(t p) d -> p t d", p=128), vr)
                self.load(nb, self.nearb.rearrange("(a h p) k -> p a h k", a=2, h=16)[:, :, h, :], nr)
                self.tt("dve", nb, nb, self.cmask, ALU.add, [nr, cst], [nr])
            cb = 4 + i // 2
            par = i % 2
            nkeys = (cb + 1) * 256
            L, Lr = Ls[idx % 3]
            Pb, Pr = Ps[idx % 3]
            rs = rs_all[:, (idx % 3) * 8:(idx % 3) * 8 + 8]
            rsr = rsrs[idx % 3]
            nbk = (nkeys + 511) // 512
            for b in range(nbk):
                w = min(512, nkeys - b * 512)
                bk, br = pbank("s")
                self.mm(bk[:, 0:w], qT[:, h, i * 128:(i + 1) * 128], kt[:, b * 512:b * 512 + w], True, True, [qr, kr], [br])
                nblk = w // 256
                self.stt(L[:, b * 512:b * 512 + w].rearrange("p (n l) -> p n l", n=nblk),
                         bk[:, 0:w].rearrange("p (n l) -> p n l", n=nblk), SCALE,
                         negb[:, i, h, 2 * b:2 * b + nblk].unsqueeze(2).broadcast_to([128, nblk, 256]),
                         ALU.mult, ALU.add, [br, nbr], [Lr])
            n0 = (cb - 1) * 256
            self.tt("dve", L[:, n0:n0 + 512], L[:, n0:n0 + 512], nb[:, par, :], ALU.add, [Lr, nr], [Lr])
            self.A(Pb[:, 0:n0], L[:, 0:n0], AF.Exp, [Lr, cst], [Pr, rsr], bias=self.cfar[:, h:h + 1], accum=rs[:, 0:1])
            self.A(Pb[:, n0:n0 + 512], L[:, n0:n0 + 512], AF.Exp, [Lr], [Pr, rsr], accum=rs[:, 1:2])

        def stageA2(idx):
            h, i = items[idx]
            nkeys = (4 + i // 2 + 1) * 256
            Pb, Pr = Ps[idx % 3]
            rs = rs_all[:, (idx % 3) * 8:(idx % 3) * 8 + 8]
            rsr = rsrs[idx % 3]
            self.tt("dve", rs[:, 2:3], rs[:, 0:1], rs[:, 1:2], ALU.add, [rsr], [rsr])
            P.op("dve", lambda e, rs=rs: e.reciprocal(out=rs[:, 3:4], in_=rs[:, 2:3]), [rsr], [rsr])
            self.ts("dve", Pb[:, 0:nkeys], Pb[:, 0:nkeys], rs[:, 3:4], None, ALU.mult, None, [Pr], [Pr], sreads=[rsr])

        def stageBT(idx):
            h, i = items[idx]
            cb = 4 + i // 2
            nkeys = (cb + 1) * 256
            Pb, Pr = Ps[idx % 3]
            PT, PTr = PTs[idx % 3]
            nchk = nkeys // 128
            for g in range((nchk + 7) // 8):
                n = min(8, nchk - g * 8)
                bk, br = pbank("t")
                bk16 = bk[:].bitcast(BF16)
                for j in range(n):
                    kc = g * 8 + j
                    self.tr(bk16[:, j * 128:(j + 1) * 128], Pb[:, kc * 128:(kc + 1) * 128], [Pr], [br])
                self.copy("act", PT[:, g * 8:g * 8 + n, :],
                          bk16[:, 0:n * 128].rearrange("p (c t) -> p c t", c=n), [br], [PTr])

        def stageBV(idx):
            h, i = items[idx]
            vt, vr = vts[h % 2]
            cb = 4 + i // 2
            nkeys = (cb + 1) * 256
            PT, PTr = PTs[idx % 3]
            nchk = nkeys // 128
            bk, br = pbank("v")
            for kc in range(nchk):
                self.mm(bk[:, 0:128], vt[:, kc, :], PT[:, kc, :], kc == 0, kc == nchk - 1, [vr, PTr], [br])
            self.copy("act", attnT[:, h, i * 128:(i + 1) * 128], bk[:, 0:128], [br], [ar])

        NI = len(items)
        stageA(0)
        stageA(1)
        stageA2(0)
        for idx in range(NI):
            stageBT(idx)
            if idx + 2 < NI:
                stageA(idx + 2)
            if idx + 1 < NI:
                stageA2(idx + 1)
            stageBV(idx)
        self.store(self.attnT_s.rearrange("(h p) t -> p h t", p=128), attnT, self.res("attnT_s"), ar)

    def phase4(self):
        P = self.P
        cst = self.res("const")
        mergedT = self.carve(16384, BF16, shape=(32, 1024))
        mr = self.res("mergedT")
        wbuf = self.carve(8192)
        ssqp = self.carve(1024)
        ssr = self.res("ssqp")
        mark = self.off
        A0 = self.carve(16384, BF16)
        attnT = A0[:, 0:16384].rearrange("p (h t) -> p h t", h=16)
        convT = A0[:, 16384:32768].rearrange("p (h t) -> p h t", h=16)
        a0r = self.res("A0")
        wbr = [(wbuf[:, i * 2048:(i + 1) * 2048].bitcast(BF16).rearrange("p (k n) -> p k n", k=16), self.res(f"wb{i}")) for i in range(4)]
        gts = [(self.carve(1024, shape=(2, 512)), self.res(f"gt{i}")) for i in range(2)]
        tmp = [(self.carve(512), self.res(f"tmp{i}")) for i in range(4)]
        self.load(attnT, self.attnT_s.rearrange("(h p) t -> p h t", p=128), a0r, reads=[self.res("attnT_s")])
        self.load(convT, self.convT_s.rearrange("(h p) t -> p h t", p=128), a0r,
                  reads=[self.res(("conv", ch)) for ch in range(16)])
        P.op("dve", lambda e: e.memset(ssqp, 0.0), [], [ssr])
        gi = 0
        for gidx, cg in enumerate(range(0, D, 256)):
            wa3, war = wbr[(gidx % 2) * 2]
            wc3, wcr = wbr[(gidx % 2) * 2 + 1]
            P.dma("pool", [(wa3, self.w_bra[:, cg:cg + 256].rearrange("(kc p) n -> p kc n", p=128))], writes=[war], key=war.name)
            P.dma("pool", [(wc3, self.w_brc[:, cg:cg + 256].rearrange("(kc p) n -> p kc n", p=128))], writes=[wcr], key=wcr.name)
            for m in range(2):
                c = cg // 128 + m
                for nh in range(2):
                    gt, gtr = gts[gi % 2]
                    gi += 1
                    tsl = slice(nh * 512, (nh + 1) * 512)
                    P.dma("sp", [(gt[:, 0, :], self.gA_s[c * 128:(c + 1) * 128, tsl]), (gt[:, 1, :], self.gC_s[c * 128:(c + 1) * 128, tsl])],
                          reads=[self.res(("g", 0, c)), self.res(("g", 1, c))], writes=[gtr], key=gtr.name)
                    ba, bar = self.bank()
                    for kc in range(16):
                        self.mm(ba[:], wa3[:, kc, m * 128:(m + 1) * 128], attnT[:, kc, tsl], kc == 0, kc == 15, [war, a0r], [bar])
                    bc_, bcr = self.bank()
                    for kc in range(16):
                        self.mm(bc_[:], wc3[:, kc, m * 128:(m + 1) * 128], convT[:, kc, tsl], kc == 0, kc == 15, [wcr, a0r], [bcr])
                    t0, t0r = tmp[(gi % 2) * 2]
                    t1, t1r = tmp[(gi % 2) * 2 + 1]
                    self.tt("dve", t0, ba[:], gt[:, 0, :], ALU.mult, [bar, gtr], [t0r])
                    self.tt("dve", t1, bc_[:], gt[:, 1, :], ALU.mult, [bcr, gtr], [t1r])
                    self.tt("dve", mergedT[:, c, tsl], t0, t1, ALU.add, [t0r, t1r], [mr])
        P.barrier()
        self.off = mark
        wts = [(wbuf[:, i * 4096:(i + 1) * 4096].bitcast(BF16).rearrange("p (k n) -> p k n", k=32), self.res(f"wo{i}")) for i in range(2)]
        xts = [(self.carve(1024), self.res(f"x4_{i}")) for i in range(2)]
        hst = [(self.carve(1024), self.res(f"h4_{i}")) for i in range(2)]
        h16 = [(self.carve(512), self.res(f"h16_{i}")) for i in range(2)]
        sq = self.carve(1024)
        sqr = self.res("sq")
        rstd = self.carve(1024)
        rr = self.res("rstd")
        for gidx, cg in enumerate(range(0, D, 256)):
            w3, wr = wts[gidx % 2]
            P.dma("pool", [(w3, self.w_out[:, cg:cg + 256].rearrange("(kc p) n -> p kc n", p=128))], writes=[wr], key=wr.name)
            for m in range(2):
                c = cg // 128 + m
                xt, xr = xts[c % 2]
                hs, hr = hst[c % 2]
                self.load(xt, self.xT[c * 128:(c + 1) * 128, :], xr)
                for nh in range(2):
                    bk, br = self.bank()
                    for kc in range(32):
                        self.mm(bk[:], w3[:, kc, m * 128:(m + 1) * 128], mergedT[:, kc, nh * 512:(nh + 1) * 512], kc == 0, kc == 31, [wr, mr], [br])
                    self.tt("dve", hs[:, nh * 512:(nh + 1) * 512], bk[:], xt[:, nh * 512:(nh + 1) * 512], ALU.add, [br, xr], [hr])
                self.A(sq, hs, AF.Square, [hr], [sqr])
                self.tt("dve", ssqp, ssqp, sq, ALU.add, [sqr, ssr], [ssr])
                self.store(self.h1T_s[c * 128:(c + 1) * 128, :], hs, self.res(("h1", c)), hr)
        for nh in range(2):
            bk, br = self.bank()
            self.mm(bk[:], self.ones32, ssqp[:, nh * 512:(nh + 1) * 512], True, True, [cst, ssr], [br])
            self.A(sq[:, nh * 512:(nh + 1) * 512], bk[:], AF.Sqrt, [br, cst], [sqr], scale=1.0 / D, bias=self.epsc)
        P.op("dve", lambda e: e.reciprocal(out=rstd, in_=sq), [sqr], [rr])
        def ld4(c):
            hs, hr = hst[c % 2]
            self.load(hs, self.h1T_s[c * 128:(c + 1) * 128, :], hr, reads=[self.res(("h1", c))])
        ld4(0)
        ld4(1)
        for c in range(32):
            hs, hr = hst[c % 2]
            hb, hbr = h16[c % 2]
            self.stt(hb.bitcast(BF16), hs, self.gffn[:, c:c + 1], rstd, ALU.mult, ALU.mult, [hr, rr, cst], [hbr])
            if c + 2 < 32:
                ld4(c + 2)
            self.store(self.hn2T_s[c * 128:(c + 1) * 128, :], hb.bitcast(BF16), self.res("hn2T_s"), hbr)

    def phase5(self):
        P = self.P
        cst = self.res("const")
        hn2 = self.carve(16384, BF16, shape=(32, 1024))
        hr_ = self.res("hn2")
        qpT = self.carve(8192, BF16, shape=(16, 1024))
        qpr = self.res("qpT")
        wts = [(self.carve(2048, BF16, shape=(32, 128)), self.res(f"wt{i}")) for i in range(2)]
        self.load(hn2, self.hn2T_s.rearrange("(c p) t -> p c t", p=128), hr_, reads=[self.res("hn2T_s")])
        wi = 0
        NCHAIN = 4
        cb_ = []
        for k in range(NCHAIN):
            cb_.append(dict(v12=self.carve(32, shape=(2, 16)), tmp128=self.carve(128), cand=self.carve(256, shape=(16, 16)),
                            cand2=self.carve(256), c24=self.carve(24), e16=self.carve(16), r=self.res(f"s5_{k}")))
        sc = self.carve(8 * 8 * 16)
        scrs = [self.res(f"sc5_{k}") for k in range(NCHAIN)]
        Es = [(self.carve(256, BF16), self.res(f"E{i}")) for i in range(4)]
        Gps = [(self.carve(2048, BF16, shape=(8, 512)), self.res(f"Gp{i}")) for i in range(2)]
        GTb = self.carve(4096, shape=(4, 1024))
        gtr = self.res("GTb")
        gl, glr = self.carve(1024), self.res("gl0")
        ast = [(self.carve(512), self.res(f"ast{i}")) for i in range(2)]
        keys = self.keys16

        def chain(tt, h, B, bkbr):
            v12, tmp128, cand, cand2, c24, e16, s5 = B["v12"], B["tmp128"], B["cand"], B["cand2"], B["c24"], B["e16"], B["r"]
            scr_ = scrs[tt % NCHAIN]
            so = (tt * 8 + h) * 16
            bk, br = bkbr
            for half in range(2):
                sv = bk[:, half * 128:(half + 1) * 128]
                P.op("dve", lambda e, half=half, sv=sv: e.max(out=v12[:, half, 0:8], in_=sv), [br], [s5], hz=True)
                yield
                P.op("dve", lambda e, half=half, sv=sv: e.match_replace(out=tmp128, in_to_replace=v12[:, half, 0:8], in_values=sv, imm_value=-1e30), [br, s5], [s5], hz=True)
                yield
                P.op("dve", lambda e, half=half, sv=sv: e.max(out=v12[:, half, 8:16], in_=tmp128), [s5], [s5], hz=True)
                yield
            self.tt("dve", cand, v12[:, 0, :].unsqueeze(2).broadcast_to([128, 16, 16]),
                    v12[:, 1, :].unsqueeze(1).broadcast_to([128, 16, 16]), ALU.add, [s5], [s5])
            yield
            candf = cand.rearrange("p a b -> p (a b)")
            P.op("dve", lambda e: e.max(out=c24[:, 0:8], in_=candf), [s5], [s5], hz=True)
            yield
            P.op("dve", lambda e: e.match_replace(out=cand2, in_to_replace=c24[:, 0:8], in_values=candf, imm_value=-1e30), [s5], [s5], hz=True)
            yield
            P.op("dve", lambda e: e.max(out=c24[:, 8:16], in_=cand2), [s5], [s5], hz=True)
            yield
            P.op("dve", lambda e: e.match_replace(out=candf, in_to_replace=c24[:, 8:16], in_values=cand2, imm_value=-1e30), [s5], [s5], hz=True)
            yield
            P.op("dve", lambda e: e.max(out=c24[:, 16:24], in_=candf), [s5], [s5], hz=True)
            yield
            s_tau = sc[:, so + 0:so + 1]
            s_negm = sc[:, so + 1:so + 2]
            s_Z = sc[:, so + 2:so + 3]
            s_lnz = sc[:, so + 3:so + 4]
            s_shift = sc[:, so + 4:so + 5]
            self.tt("dve", s_tau, c24[:, 15:16], c24[:, 16:17], ALU.add, [s5], [scr_])
            yield
            self.ts("dve", s_tau, s_tau, 0.5, None, ALU.mult, None, [scr_], [scr_])
            yield
            self.ts("dve", s_negm, c24[:, 0:1], -1.0, None, ALU.mult, None, [s5], [scr_])
            yield
            self.A(e16, c24[:, 0:16], AF.Exp, [s5, scr_], [s5, scr_], bias=s_negm, accum=s_Z)
            yield
            self.A(s_lnz, s_Z, AF.Ln, [scr_], [scr_])
            yield
            self.tt("dve", s_shift, s_negm, s_lnz, ALU.subtract, [scr_], [scr_])
            yield

        for h in range(8):
            for c in (2 * h, 2 * h + 1):
                w3, wr = wts[wi % 2]
                wi += 1
                P.dma("pool", [(w3, self.w_q[:, c * 128:(c + 1) * 128].rearrange("(kc p) n -> p kc n", p=128))], writes=[wr], key=wr.name)
                for nh in range(2):
                    bk, br = self.bank()
                    for kc in range(32):
                        self.mm(bk[:], w3[:, kc, :], hn2[:, kc, nh * 512:(nh + 1) * 512], kc == 0, kc == 31, [wr, hr_], [br])
                    self.copy(self.eveng(), qpT[:, c, nh * 512:(nh + 1) * 512], bk[:], [br], [qpr])
            for t0 in range(0, 8, NCHAIN):
                gens = []
                for k in range(NCHAIN):
                    tt = t0 + k
                    tsl = slice(tt * 128, (tt + 1) * 128)
                    bk, br = self.bank()
                    for half in range(2):
                        self.mm(bk[:, half * 128:(half + 1) * 128], qpT[:, 2 * h + half, tsl], keys[:, 2 * h + half, :], True, True, [qpr, cst], [br])
                    gens.append(chain(tt, h, cb_[k], (bk, br)))
                while gens:
                    for g in list(gens):
                        try:
                            next(g)
                        except StopIteration:
                            gens.remove(g)
        GTbs = [(GTb, gtr), (self.carve(4096, shape=(4, 1024)), self.res("GTb1"))]

        bctr5 = {"s": 0, "u": 0, "t": 0}

        def pbank5(kind):
            lo, cnt = {"s": (0, 5), "u": (5, 2), "t": (7, 1)}[kind]
            b = self.banks[lo + bctr5[kind] % cnt]
            bctr5[kind] += 1
            return b

        def front(n):
            ig, tt = n // 8, n % 8
            tsl = slice(tt * 128, (tt + 1) * 128)
            Gp, Gpr = Gps[n % 2]
            for h in range(8):
                so = (tt * 8 + h) * 16
                scr_ = scrs[tt % NCHAIN]
                E, Er = Es[h % 4]
                bk, br = self.bank()
                o3 = bk[:].rearrange("p (i j) -> p i j", i=4)
                self.mm(o3, qpT[:, 2 * h, tsl], keys[:, 2 * h, ig * 4:(ig + 1) * 4].unsqueeze(2).broadcast_to([128, 4, 128]),
                        True, False, [qpr, cst], [br])
                self.mm(o3, qpT[:, 2 * h + 1, tsl], keys[:, 2 * h + 1, :].unsqueeze(1).broadcast_to([128, 4, 128]),
                        False, True, [qpr, cst], [br])
                self.A(E, bk[:], AF.Exp, [br, scr_], [Er], bias=sc[:, so + 4:so + 5])
                self.stt(Gp[:, h, :], bk[:], sc[:, so:so + 1], E, ALU.is_ge, ALU.mult, [br, Er], [Gpr], sreads=[scr_])
            self.tt("dve", Gp[:, 0:4, :], Gp[:, 0:4, :], Gp[:, 4:8, :], ALU.add, [Gpr], [Gpr])
            self.tt("dve", Gp[:, 0:2, :], Gp[:, 0:2, :], Gp[:, 2:4, :], ALU.add, [Gpr], [Gpr])
            self.tt("dve", Gp[:, 0, :], Gp[:, 0, :], Gp[:, 1, :], ALU.add, [Gpr], [Gpr])

        def back(n):
            ig, tt = n // 8, n % 8
            tsl = slice(tt * 128, (tt + 1) * 128)
            Gp, Gpr = Gps[n % 2]
            gb_, gbr_ = GTbs[ig % 2]
            bk, br = self.bank()
            for ii in range(4):
                self.mm(bk[:, ii * 128:(ii + 1) * 128], Gp[:, 0, ii * 128:(ii + 1) * 128], self.ident16, True, True, [Gpr, cst], [br])
            self.copy("act", gb_[:, :, tsl], bk[:].rearrange("p (i t) -> p i t", i=4), [br], [gbr_])

        ustate = {}

        def ugrp(ig, k):
            nonlocal wi
            q, nh = k // 2, k % 2
            i = ig * 4 + q
            gb_, gbr_ = GTbs[ig % 2]
            if nh == 0:
                w3, wr = wts[wi % 2]
                wi += 1
                P.dma("pool", [(w3, self.uT[:, i * 128:(i + 1) * 128].rearrange("(kc p) n -> p kc n", p=128))], writes=[wr], key=wr.name)
                ustate["w"] = (w3, wr)
            w3, wr = ustate["w"]
            bk, br = self.bank()
            for kc in range(32):
                self.mm(bk[:], w3[:, kc, :], hn2[:, kc, nh * 512:(nh + 1) * 512], kc == 0, kc == 31, [wr, hr_], [br])
            self.A(gl[:, nh * 512:(nh + 1) * 512], bk[:], AF.Gelu, [br], [glr])
            if nh == 1:
                a16, a16r = ast[i % 2]
                self.tt("dve", a16.bitcast(BF16), gl, gb_[:, q, :], ALU.mult, [glr, gbr_], [a16r])
                self.store(self.actT_s[i * 128:(i + 1) * 128, :], a16.bitcast(BF16), self.res(("act", i)), a16r)

        NS = 32 * 8
        for n in range(NS):
            front(n)
            if n >= 8:
                ugrp(n // 8 - 1, n % 8)
            if n >= 1:
                back(n - 1)
        back(NS - 1)
        for k in range(8):
            ugrp(31, k)

    def phase6(self):
        P = self.P
        cst = self.res("const")
        acts = [(self.carve(4096, BF16, shape=(8, 1024)), self.res(f"ac{i}")) for i in range(3)]
        vts = [(self.carve(2048, BF16, shape=(8, 512)), self.res(f"vw{i}")) for i in range(3)]
        hst = [(self.carve(1024), self.res(f"h6_{i}")) for i in range(2)]
        ost = [(self.carve(1024), self.res(f"o6_{i}")) for i in range(2)]
        sq = self.carve(1024)
        sqr = self.res("sq6")
        ssqp = self.carve(1024)
        ssr = self.res("ssq6")
        rstd = self.carve(1024)
        rr = self.res("rstd6")
        P.op("dve", lambda e: e.memset(ssqp, 0.0), [], [ssr])
        NRES = 5
        resid = [(self.carve(4096, BF16, shape=(8, 1024)), self.res(f"acres{i}")) for i in range(NRES)]
        li = 0
        for ps in range(8):
            col0 = ps * 512
            bks = [self.bank() for _ in range(8)]
            for cc in range(2):
                c = ps * 4 + cc
                hs, hr = hst[c % 2]
                self.load(hs, self.h1T_s[c * 128:(c + 1) * 128, :], hr)
            for eg in range(16):
                vw, vwr = vts[li % 3]
                e0 = eg * 8
                if eg < NRES:
                    ac, acr = resid[eg]
                else:
                    ac, acr = acts[li % 3]
                li += 1
                if eg >= NRES or ps == 0:
                    P.dma("sp", [(ac, self.actT_s[e0 * 128:(e0 + 8) * 128, :].rearrange("(e p) t -> p e t", p=128))],
                          reads=[self.res(("act", e0 + k)) for k in range(8)] if ps == 0 else (), writes=[acr], key=acr.name)
                P.dma("pool", [(vw, self.pv[e0 * 128:(e0 + 8) * 128, col0:col0 + 512].rearrange("(e p) n -> p e n", p=128))],
                      writes=[vwr], key=vwr.name)
                for k in range(8):
                    e = e0 + k
                    for cc in range(4):
                        for nh in range(2):
                            bk, br = bks[cc * 2 + nh]
                            self.mm(bk[:], vw[:, k, cc * 128:(cc + 1) * 128], ac[:, k, nh * 512:(nh + 1) * 512], e == 0, e == 127, [vwr, acr], [br])
            for cc in range(4):
                c = ps * 4 + cc
                hs, hr = hst[c % 2]
                os_, osr = ost[c % 2]
                if cc >= 2:
                    self.load(hs, self.h1T_s[c * 128:(c + 1) * 128, :], hr)
                for nh in range(2):
                    bk, br = bks[cc * 2 + nh]
                    self.tt("dve", os_[:, nh * 512:(nh + 1) * 512], bk[:], hs[:, nh * 512:(nh + 1) * 512], ALU.add, [br, hr], [osr])
                self.A(sq, os_, AF.Square, [osr], [sqr])
                self.tt("dve", ssqp, ssqp, sq, ALU.add, [sqr, ssr], [ssr])
                self.store(self.h2T_s[c * 128:(c + 1) * 128, :], os_, self.res(("h2", c)), osr)
        for nh in range(2):
            bk, br = self.bank()
            self.mm(bk[:], self.ones32, ssqp[:, nh * 512:(nh + 1) * 512], True, True, [cst, ssr], [br])
            self.A(sq[:, nh * 512:(nh + 1) * 512], bk[:], AF.Sqrt, [br, cst], [sqr], scale=1.0 / D, bias=self.epsc)
        P.op("dve", lambda e: e.reciprocal(out=rstd, in_=sq), [sqr], [rr])
        def ld6(c):
            hs, hr = hst[c % 2]
            self.load(hs, self.h2T_s[c * 128:(c + 1) * 128, :], hr, reads=[self.res(("h2", c))])
        ld6(0)
        ld6(1)
        for c in range(32):
            hs, hr = hst[c % 2]
            os_, osr = ost[c % 2]
            self.stt(os_, hs, self.gfin[:, c:c + 1], rstd, ALU.mult, ALU.mult, [hr, rr, cst], [osr])
            if c + 2 < 32:
                ld6(c + 2)
            self.store(self.outT[c * 128:(c + 1) * 128, :], os_, self.res(("out", c)), osr)


def _t5_bucket(dist):
    dist = np.asarray(dist, dtype=np.int64)
    max_exact = 16
    d32 = np.maximum(dist, 1).astype(np.float32)
    val = np.log(d32 / np.float32(max_exact)) / np.float32(math.log(128 / max_exact)) * np.float32(32 - max_exact)
    large = max_exact + val.astype(np.int32)
    large = np.minimum(large, 31)
    return np.where(dist < max_exact, dist, large)


def _pcol(vec, C):
    return np.ascontiguousarray(np.asarray(vec, np.float32).reshape(C, 128).T)


def prepare_inputs(x, norm_mix, w_in, conv_w, conv_b, w_br_attn, w_br_conv, b_gate, rel_bias,
                   w_out, norm_ffn, peer_w_q, peer_sub_keys, peer_u, peer_v, norm_final):
    f = lambda a: np.ascontiguousarray(np.asarray(a, dtype=np.float32))
    x = f(x)
    rel_bias = f(rel_bias)
    shared = {
        "w_in": f(w_in[0]), "w_bra": f(w_br_attn[0]), "w_brc": f(w_br_conv[0]), "w_out": f(w_out[0]),
        "w_q": f(peer_w_q[0]), "uT": np.ascontiguousarray(f(peer_u[0]).T), "pv": f(peer_v[0]),
        "keysT": np.ascontiguousarray(np.transpose(f(peer_sub_keys[0]), (3, 0, 1, 2)).reshape(128, 16 * 128)),
        "gmixb": f(norm_mix[0]).reshape(1, D),
        "identd": np.eye(128, dtype=np.float32),
    }
    qi = np.arange(128)[:, None]
    kj = np.arange(512)[None, :]
    nearb = np.zeros((2, 16, 128, 512), np.float32)
    cmask = np.zeros((128, 2, 512), np.float32)
    for par in range(2):
        dist = (256 + par * 128 + qi) - kj
        bk = _t5_bucket(np.maximum(dist, 0))
        nearb[par] = np.transpose(rel_bias[bk], (2, 0, 1))
        cmask[:, par, :] = np.where(dist >= 0, 0.0, BIGNEG)
    shared["nearb"] = np.ascontiguousarray(nearb.reshape(2 * 16 * 128, 512))
    cw = f(conv_w[0])[:, 0, :]
    convw = np.ascontiguousarray(np.transpose(cw.reshape(3, 16, 128), (2, 1, 0))).reshape(128, 48)
    bg = np.ascontiguousarray(np.transpose(f(b_gate[0]).reshape(2, 32, 128), (2, 0, 1))).reshape(128, 64)
    in_maps = []
    for c in range(8):
        b, half = c // 2, c % 2
        own = x[b, half * 1024:(half + 1) * 1024]
        prev = x[b, 0:1024] if half == 1 else np.zeros_like(own)
        pastb = np.zeros((128, 8, 8), np.float32)
        valid = np.zeros((128, 8, 8), np.float32)
        ownm = np.zeros((128, 8, 8), np.float32)
        for i in range(8):
            cb = 4 + i // 2
            for n in range(8):
                ok = (n < cb) and (half == 1 or n >= 4)
                pastb[:, i, n] = 0.0 if ok else -1e30
                valid[:, i, n] = 1.0 if ok else 0.0
                ownm[:, i, n] = 1.0 if n == cb else 0.0
        small = np.concatenate([
            _pcol(norm_mix[0], 32), _pcol(norm_ffn[0], 32), _pcol(norm_final, 32),
            convw, _pcol(conv_b[0], 16), bg,
            np.tile(rel_bias[31][None, :], (128, 1)),
            pastb.reshape(128, 64), valid.reshape(128, 64), ownm.reshape(128, 64),
            cmask.reshape(128, 1024), np.ones((128, 128), np.float32), np.full((128, 1), EPS, np.float32),
        ], axis=1)
        pad = np.zeros((128, 1600 - small.shape[1]), np.float32)
        m = dict(shared)
        m["xctx"] = np.ascontiguousarray(np.concatenate([prev, own], axis=0))
        m["xT"] = np.ascontiguousarray(own.T)
        m["smallc"] = np.ascontiguousarray(np.concatenate([small, pad], axis=1))
        in_maps.append(m)
    return in_maps


def kernel(**inputs):
    in_maps = prepare_inputs(**inputs)
    nc = Builder().build()
    res = run_bass_kernel_spmd(nc, in_maps, core_ids=list(range(8)))
    out = np.empty((4, 2048, D), np.float32)
    for c in range(8):
        b, half = c // 2, c % 2
        out[b, half * 1024:(half + 1) * 1024, :] = res.results[c]["outT"].T
    return out
```
